# Optimizing a Trainium2 kernel written in Bass

```python
import functools
import jax, jax.numpy as jnp
from jax import lax
import numpy as np

D_MODEL = 2048
BATCH = 2
SEQ = 4096
DEPTH = 1
DEC_BATCH = 32
DEC_SEQ = 4
PAST_LEN = 8192
PAGE_SIZE = 128

D_MIX = D_MODEL
D_RNN = D_MIX // 2
LRU_BLOCKS = 8
LRU_BW = D_RNN // LRU_BLOCKS
CONV_W = 4
LRU_C = 8.0
HEAD_DIM = 128
N_Q_HEADS = (D_MIX - D_RNN) // HEAD_DIM
N_KV_HEADS = 4
GQA_GROUP = N_Q_HEADS // N_KV_HEADS
IDX_HEADS = 8
IDX_DIM = 64
TOPK_MAX = 256
Q_BLOCK = 128
D_FF = 5504
LN_EPS = 1e-5
ALPHA = (2.0 * DEPTH) ** 0.25
BETA = (8.0 * DEPTH) ** -0.25
ATTN_SCALE = HEAD_DIM ** -0.5
IDX_SCALE = IDX_DIM ** -0.5
IDX_W_SCALE = IDX_HEADS ** -0.5
SPLIT_SIZES = (D_RNN, D_RNN, N_Q_HEADS * HEAD_DIM, N_KV_HEADS * HEAD_DIM,
               N_KV_HEADS * HEAD_DIM, IDX_HEADS * IDX_DIM, IDX_DIM, IDX_HEADS)
D_IN = sum(SPLIT_SIZES)
SPLIT_POINTS = tuple(int(s) for s in np.cumsum(SPLIT_SIZES)[:-1])

kernel_name = 'hymba_rglru_dsa_macaron_step'


def layer_norm(x, g, b):
    xf = x.astype(jnp.float32)
    mu = jnp.mean(xf, axis=-1, keepdims=True)
    var = jnp.mean(jnp.square(xf - mu), axis=-1, keepdims=True)
    return ((xf - mu) * lax.rsqrt(var + LN_EPS) * g + b).astype(x.dtype)


def swiglu(x, w_gu, w_down):
    gate, up = jnp.split(x @ w_gu, 2, axis=-1)
    return (jax.nn.silu(gate) * up) @ w_down


def causal_conv(x, buf, w, b):
    T = x.shape[1]
    xp = jnp.concatenate([buf.astype(x.dtype), x], axis=1)
    out = b + w[0] * xp[:, 0:T]
    for j in range(1, CONV_W):
        out = out + w[j] * xp[:, j:j + T]
    return out, xp[:, -(CONV_W - 1):]


def rg_lru(xc, h0, w_a, b_a, w_i, b_i, lam):
    B, T, _ = xc.shape
    xb = xc.reshape(B, T, LRU_BLOCKS, LRU_BW)
    r = jax.nn.sigmoid((jnp.einsum('btnc,ncd->btnd', xb, w_a).reshape(B, T, D_RNN) + b_a).astype(jnp.float32))
    i = jax.nn.sigmoid((jnp.einsum('btnc,ncd->btnd', xb, w_i).reshape(B, T, D_RNN) + b_i).astype(jnp.float32))
    log_a = -LRU_C * r * jax.nn.softplus(-lam.astype(jnp.float32))
    a = jnp.exp(log_a)
    u = jnp.sqrt(-jnp.expm1(2.0 * log_a)) * i * xc.astype(jnp.float32)
    u = u.at[:, 0].add(a[:, 0] * h0.astype(jnp.float32))

    def combine(left, right):
        return (left[0] * right[0], right[0] * left[1] + right[1])

    _, h = lax.associative_scan(combine, (a, u), axis=1)
    return h, h[:, -1]


def indexer_scores(qi, wi, ki):
    s = jax.nn.relu(jnp.einsum('bqhd,bld->bqhl', qi, ki).astype(jnp.float32) * IDX_SCALE)
    return jnp.einsum('bqhl,bqh->bql', s, wi.astype(jnp.float32) * IDX_W_SCALE)


def attend_selected(q, kg, vg, valid):
    B, Q = q.shape[:2]
    qg = q.reshape(B, Q, N_KV_HEADS, GQA_GROUP, HEAD_DIM)
    logits = jnp.einsum('bqngd,bqsnd->bqngs', qg, kg).astype(jnp.float32) * ATTN_SCALE
    logits = jnp.where(valid[:, :, None, None, :], logits, -jnp.inf)
    p = jax.nn.softmax(logits, axis=-1).astype(vg.dtype)
    o = jnp.einsum('bqngs,bqsnd->bqngd', p, vg)
    return o.reshape(B, Q, N_Q_HEADS * HEAD_DIM)


def select_and_attend(q, scores, admissible, t_pos, topk, gather):
    scores = jnp.where(admissible[None], scores, -jnp.inf)
    _, idx = lax.top_k(scores, topk)
    valid = idx <= t_pos[None, :, None]
    kg, vg = gather(idx)
    return attend_selected(q, kg, vg, valid)


def prompt_sparse_attention(q, k, v, qi, ki, wi):
    B, S = q.shape[:2]
    topk = min(TOPK_MAX, S // 4)
    n_blk = S // Q_BLOCK
    key_pos = jnp.arange(S)

    def to_blocks(a):
        return jnp.moveaxis(a.reshape((B, n_blk, Q_BLOCK) + a.shape[2:]), 1, 0)

    def gather(idx):
        take = jax.vmap(lambda rows, ii: rows[ii])
        return take(k, idx), take(v, idx)

    def one_block(args):
        qb, qib, wib, blk = args
        t = blk * Q_BLOCK + jnp.arange(Q_BLOCK)
        scores = indexer_scores(qib, wib, ki)
        admissible = key_pos[None, :] <= t[:, None]
        return select_and_attend(qb, scores, admissible, t, topk, gather)

    out = lax.map(one_block, (to_blocks(q), to_blocks(qi), to_blocks(wi), jnp.arange(n_blk)))
    return jnp.moveaxis(out, 0, 1).reshape(B, S, N_Q_HEADS * HEAD_DIM)


def sample_sparse_attention(q, k, v, qi, ki, wi, cache_k, cache_v, cache_k_idx, page_table):
    B, T = q.shape[:2]
    n_past = page_table.shape[1] * PAGE_SIZE
    L = n_past + T
    topk = min(TOPK_MAX, L // 4)
    past_ki = cache_k_idx[page_table].reshape(B, n_past, IDX_DIM)
    keys_idx = jnp.concatenate([past_ki, ki.astype(past_ki.dtype)], axis=1)
    t = n_past + jnp.arange(T)
    scores = indexer_scores(qi, wi, keys_idx)
    admissible = jnp.arange(L)[None, :] <= t[:, None]

    def gather(idx):
        in_past = idx < n_past
        pidx = jnp.minimum(idx, n_past - 1)
        phys = jax.vmap(lambda pt, ii: pt[ii])(page_table, pidx // PAGE_SIZE)
        off = pidx % PAGE_SIZE
        nidx = jnp.clip(idx - n_past, 0, T - 1)
        take = jax.vmap(lambda rows, ii: rows[ii])

        def pick(pool, new):
            past_rows = pool[phys, off]
            new_rows = take(new, nidx).astype(past_rows.dtype)
            return jnp.where(in_past[..., None, None], past_rows, new_rows)

        return pick(cache_k, k), pick(cache_v, v)

    return select_and_attend(q, scores, admissible, t, topk, gather)


def decoder_layer(x, attn_fn, conv_buf, h0, ln1_g, ln1_b, ffn1_w_gu, ffn1_w_down, w_in,
                  conv_w, conv_b, lru_w_a, lru_b_a, lru_w_i, lru_b_i, lru_lambda, w_out,
                  ln2_g, ln2_b, ffn2_w_gu, ffn2_w_down, ln3_g, ln3_b):
    B, T, _ = x.shape
    x = layer_norm(ALPHA * x + 0.5 * swiglu(x, ffn1_w_gu, ffn1_w_down), ln1_g, ln1_b)
    xr, gr, q, k, v, qi, ki, wi = jnp.split(x @ w_in, SPLIT_POINTS, axis=-1)
    xc, conv_state = causal_conv(xr, conv_buf, conv_w, conv_b)
    h, h_last = rg_lru(xc, h0, lru_w_a, lru_b_a, lru_w_i, lru_b_i, lru_lambda)
    rnn_out = h.astype(x.dtype) * jax.nn.gelu(gr)
    q = q.reshape(B, T, N_Q_HEADS, HEAD_DIM)
    k = k.reshape(B, T, N_KV_HEADS, HEAD_DIM)
    v = v.reshape(B, T, N_KV_HEADS, HEAD_DIM)
    qi = qi.reshape(B, T, IDX_HEADS, IDX_DIM)
    attn_out = attn_fn(q, k, v, qi, ki, wi)
    mix = jnp.concatenate([rnn_out, attn_out.astype(x.dtype)], axis=-1) @ w_out
    x = layer_norm(ALPHA * x + mix, ln2_g, ln2_b)
    x = layer_norm(ALPHA * x + 0.5 * swiglu(x, ffn2_w_gu, ffn2_w_down), ln3_g, ln3_b)
    return x, k, v, ki, conv_state, h_last


def setup_inputs(seed: int = 0) -> dict:
    key = jax.random.key(seed)
    ks = jax.random.split(key, 32)
    n_pages = PAST_LEN // PAGE_SIZE
    n_used = DEC_BATCH * n_pages
    n_pool = n_used + n_used // 4

    def nrm(k, shape, scale):
        return jax.random.normal(k, shape, jnp.float32) * scale

    u = jax.random.uniform(ks[19], (DEPTH, D_RNN), jnp.float32, minval=0.9, maxval=0.999)
    s = u ** (1.0 / LRU_C)
    return {
        'x_prompt': nrm(ks[0], (BATCH, SEQ, D_MODEL), 1.0),
        'x_sample': nrm(ks[1], (DEC_BATCH, DEC_SEQ, D_MODEL), 1.0),
        'cache_k': nrm(ks[2], (DEPTH, n_pool, PAGE_SIZE, N_KV_HEADS, HEAD_DIM), 1.0),
        'cache_v': nrm(ks[3], (DEPTH, n_pool, PAGE_SIZE, N_KV_HEADS, HEAD_DIM), 1.0),
        'cache_k_idx': nrm(ks[4], (DEPTH, n_pool, PAGE_SIZE, IDX_DIM), 1.0),
        'state_conv': nrm(ks[5], (DEPTH, DEC_BATCH, CONV_W - 1, D_RNN), 1.0),
        'state_rnn': nrm(ks[6], (DEPTH, DEC_BATCH, D_RNN), 0.5),
        'page_table': jax.random.permutation(ks[7], n_pool)[:n_used].reshape(DEC_BATCH, n_pages).astype(jnp.int32),
        'ln1_g': 1.0 + nrm(ks[8], (DEPTH, D_MODEL), 0.01),
        'ln1_b': nrm(ks[9], (DEPTH, D_MODEL), 0.01),
        'ffn1_w_gu': nrm(ks[10], (DEPTH, D_MODEL, 2 * D_FF), D_MODEL ** -0.5),
        'ffn1_w_down': nrm(ks[11], (DEPTH, D_FF, D_MODEL), BETA * D_FF ** -0.5),
        'w_in': nrm(ks[12], (DEPTH, D_MODEL, D_IN), D_MODEL ** -0.5),
        'conv_w': nrm(ks[13], (DEPTH, CONV_W, D_RNN), CONV_W ** -0.5),
        'conv_b': nrm(ks[14], (DEPTH, D_RNN), 0.01),
        'lru_w_a': nrm(ks[15], (DEPTH, LRU_BLOCKS, LRU_BW, LRU_BW), LRU_BW ** -0.5),
        'lru_b_a': nrm(ks[16], (DEPTH, D_RNN), 0.01),
        'lru_w_i': nrm(ks[17], (DEPTH, LRU_BLOCKS, LRU_BW, LRU_BW), LRU_BW ** -0.5),
        'lru_b_i': nrm(ks[18], (DEPTH, D_RNN), 0.01),
        'lru_lambda': jnp.log(s) - jnp.log1p(-s),
        'w_out': nrm(ks[20], (DEPTH, D_MIX, D_MODEL), BETA * D_MIX ** -0.5),
        'ln2_g': 1.0 + nrm(ks[21], (DEPTH, D_MODEL), 0.01),
        'ln2_b': nrm(ks[22], (DEPTH, D_MODEL), 0.01),
        'ffn2_w_gu': nrm(ks[23], (DEPTH, D_MODEL, 2 * D_FF), D_MODEL ** -0.5),
        'ffn2_w_down': nrm(ks[24], (DEPTH, D_FF, D_MODEL), BETA * D_FF ** -0.5),
        'ln3_g': 1.0 + nrm(ks[25], (DEPTH, D_MODEL), 0.01),
        'ln3_b': nrm(ks[26], (DEPTH, D_MODEL), 0.01),
    }


def reference(x_prompt, x_sample, cache_k, cache_v, cache_k_idx, state_conv, state_rnn, page_table,
              ln1_g, ln1_b, ffn1_w_gu, ffn1_w_down, w_in, conv_w, conv_b, lru_w_a, lru_b_a,
              lru_w_i, lru_b_i, lru_lambda, w_out, ln2_g, ln2_b, ffn2_w_gu, ffn2_w_down, ln3_g, ln3_b):
    B = x_prompt.shape[0]
    y_p, y_s = x_prompt, x_sample
    kp_l, vp_l, kip_l, cp_l, hp_l = [], [], [], [], []
    ks_l, vs_l, kis_l, cs_l, hs_l = [], [], [], [], []
    for l in range(DEPTH):
        lw = dict(ln1_g=ln1_g[l], ln1_b=ln1_b[l], ffn1_w_gu=ffn1_w_gu[l], ffn1_w_down=ffn1_w_down[l],
                  w_in=w_in[l], conv_w=conv_w[l], conv_b=conv_b[l], lru_w_a=lru_w_a[l],
                  lru_b_a=lru_b_a[l], lru_w_i=lru_w_i[l], lru_b_i=lru_b_i[l],
                  lru_lambda=lru_lambda[l], w_out=w_out[l], ln2_g=ln2_g[l], ln2_b=ln2_b[l],
                  ffn2_w_gu=ffn2_w_gu[l], ffn2_w_down=ffn2_w_down[l], ln3_g=ln3_g[l], ln3_b=ln3_b[l])
        y_p, kp, vp, kip, cp, hp = decoder_layer(
            y_p, prompt_sparse_attention,
            jnp.zeros((B, CONV_W - 1, D_RNN), y_p.dtype), jnp.zeros((B, D_RNN), jnp.float32), **lw)
        sample_attn = functools.partial(sample_sparse_attention, cache_k=cache_k[l], cache_v=cache_v[l],
                                        cache_k_idx=cache_k_idx[l], page_table=page_table)
        y_s, ksm, vsm, kism, csm, hsm = decoder_layer(y_s, sample_attn, state_conv[l], state_rnn[l], **lw)
        kp_l.append(kp); vp_l.append(vp); kip_l.append(kip); cp_l.append(cp); hp_l.append(hp)
        ks_l.append(ksm); vs_l.append(vsm); kis_l.append(kism); cs_l.append(csm); hs_l.append(hsm)
    return (y_p, y_s,
            jnp.stack(kp_l), jnp.stack(vp_l), jnp.stack(kip_l), jnp.stack(cp_l), jnp.stack(hp_l),
            jnp.stack(ks_l), jnp.stack(vs_l), jnp.stack(kis_l), jnp.stack(cs_l), jnp.stack(hs_l))
```

```python
import numpy as np
import ml_dtypes
from contextlib import ExitStack

import concourse.bass as bass
import concourse.mybir as mybir
from concourse.bass_utils import run_bass_kernel_spmd

F32 = mybir.dt.float32
BF16 = mybir.dt.bfloat16
I32 = mybir.dt.int32
ALU = mybir.AluOpType
AF = mybir.ActivationFunctionType
AX = mybir.AxisListType

D = 2048
KC = D // 128
DFF = 5504
FC = DFF // 128
DIN = 4680
DRNN = 1024
NPB = 1024
NH = 24
NS = 16
NT = NPB + NH + NS
HALO0 = NPB
SMP0 = NPB + NH
TILES = [(0, 512), (512, 512), (1024, NT - 1024)]
ALPHA = 2.0 ** 0.25
LN_EPS = 1e-5
NEG = -30000.0


class Prog:
    ENGS = ("pe", "act", "dve", "pool", "sp")

    def __init__(self, nc):
        self.nc = nc
        self.ops = []
        self.last_w = {}
        self.readers = {}
        self.dma_count = {}
        self.cur_barrier = None

    def op(self, eng, fn, r=(), w=(), dma=None, nobarrier=False, cc=False):
        idx = len(self.ops)
        deps = set()
        if self.cur_barrier is not None and not nobarrier:
            deps |= self.cur_barrier
        w = list(w) + [t for t in r if t.startswith("ps") and t not in w]
        mykey = ("d", dma) if dma is not None else ("e", eng)
        for t in r:
            if t in self.last_w:
                deps.add(self.last_w[t])
        for t in w:
            if t in self.last_w:
                deps.add(self.last_w[t])
            for x in self.readers.get(t, {}).values():
                deps.add(x)
        for t in r:
            self.readers.setdefault(t, {})[mykey] = idx
        for t in w:
            self.last_w[t] = idx
            self.readers[t] = {}
        deps.discard(idx)
        seq = None
        if dma is not None:
            self.dma_count[dma] = self.dma_count.get(dma, 0) + 1
            seq = self.dma_count[dma]
        self.ops.append(dict(eng=eng, fn=fn, deps=sorted(deps), dma=dma, seq=seq,
                             sig=False, sigval=None, inc=(1 if cc else 16)))
        return idx

    def emit(self, es):
        nc = self.nc
        ops = self.ops
        for o in ops:
            for d in o["deps"]:
                p = ops[d]
                if p["dma"] is not None:
                    continue
                if p["eng"] == o["eng"] and p["eng"] == "pe":
                    continue
                p["sig"] = True
        cnt = {e: 0 for e in self.ENGS}
        for o in ops:
            if o["dma"] is None and o["sig"]:
                cnt[o["eng"]] += 1
                o["sigval"] = cnt[o["eng"]]
        esem = {e: es.enter_context(nc.semaphore("sem_" + e)) for e in self.ENGS}
        dsem = {k: es.enter_context(nc.semaphore("dma_" + str(i)))
                for i, k in enumerate(sorted(self.dma_count))}
        block = es.enter_context(nc.Block())
        by_eng = {e: [o for o in ops if o["eng"] == e] for e in self.ENGS}

        def run(eng, e):
            waited = {}
            for o in by_eng[eng]:
                need = {}
                for d in o["deps"]:
                    p = ops[d]
                    if p["dma"] is not None:
                        sem, val, key = dsem[p["dma"]], p["inc"] * p["seq"], ("d", p["dma"])
                    else:
                        if p["eng"] == eng and eng == "pe":
                            continue
                        sem, val, key = esem[p["eng"]], p["sigval"], ("e", p["eng"])
                    if waited.get(key, 0) >= val:
                        continue
                    if key not in need or need[key][1] < val:
                        need[key] = (sem, val)
                for key, (sem, val) in need.items():
                    e.wait_ge(sem, val)
                    waited[key] = val
                ins = o["fn"](e)
                if ins is None:
                    continue
                if o["dma"] is not None:
                    ins.then_inc(dsem[o["dma"]], o["inc"])
                elif o["sig"]:
                    ins.then_inc(esem[eng], 1)

        @block.tensor
        def _(e):
            run("pe", e)

        @block.scalar
        def _(e):
            run("act", e)

        @block.vector
        def _(e):
            run("dve", e)

        @block.gpsimd
        def _(e):
            run("pool", e)

        @block.sync
        def _(e):
            run("sp", e)

    def barrier(self):
        last = {}
        for i, o in enumerate(self.ops):
            k = ("d", o["dma"]) if o["dma"] is not None else ("e", o["eng"])
            last[k] = i
        self.cur_barrier = set(last.values())


AW = 52000
NPV = 176
PV_LN = 0
PV_LNA = 96
NPV2 = 80
PV2_CONVW = 0
PV2_CONVB = 32
PV2_BA = 40
PV2_BI = 48
PV2_LAM = 56
PV2_SP = 64


def _prod(s):
    r = 1
    for v in s:
        r *= v
    return r


class Builder:
    def __init__(self, stage=99, upto=None, skip_ffn=False):
        self.stage = stage
        self.upto = upto
        self.skip_ffn = skip_ffn
        self.gcount = 0
        self.tiles = TILES
        self.weights_nobarrier = True
        self.otoks = []
        self.gtoks = []
        self.nc = nc = bass.Bass("TRN2", target_bir_lowering=False)
        self.es = ExitStack()
        self.P = Prog(nc)
        self.dram = {}
        self.A = self.es.enter_context(nc.sbuf_tensor("arena", [128, AW], F32))
        self.ps = [self.es.enter_context(nc.psum_tensor(f"ps{i}", [128, 512], F32)) for i in range(8)]

    def din(self, name, shape, dt=F32):
        t = self.nc.dram_tensor(name, list(shape), dt, kind="ExternalInput").ap()
        self.dram[name] = t
        return t

    def dout(self, name, shape, dt=F32):
        t = self.nc.dram_tensor(name, list(shape), dt, kind="ExternalOutput").ap()
        self.dram[name] = t
        return t

    def view(self, off, shape, dt=F32, parts=128):
        n = _prod(shape)
        words = n if dt in (F32, I32) else (n + 1) // 2
        assert off + words <= AW, (off, words)
        ap = self.A[0:parts, off:off + words]
        if dt != F32:
            ap = ap.bitcast(dt)
        if len(shape) == 2:
            ap = ap.rearrange("p (a b) -> p a b", a=shape[0])
        elif len(shape) == 3:
            ap = ap.rearrange("p (a b c) -> p a b c", a=shape[0], b=shape[1])
        elif len(shape) == 4:
            ap = ap.rearrange("p (a b c d) -> p a b c d", a=shape[0], b=shape[1], c=shape[2])
        return ap

    def otok(self, k):
        t = f"out_{k}_{len(self.otoks)}"
        self.otoks.append(t)
        return t

    def gtok(self):
        t = f"gin_{len(self.gtoks)}"
        self.gtoks.append(t)
        return t

    def small(self, name, shape, dt=F32):
        return self.es.enter_context(self.nc.sbuf_tensor(name, list(shape), dt))

    def setup_common(self):
        P = self.P
        self.pvec_d = self.din("pvec", [128, NPV])
        self.pvec2_d = self.din("pvec2", [128, NPV2])
        self.cvec_d = self.din("cvec", [128, 96])
        self.ident_d = self.din("ident", [128, 128])
        self.identb_d = self.din("identb", [128, 256], BF16)
        self.pvec = self.small("pvec_s", [128, NPV])
        self.pvec2 = self.small("pvec2_s", [128, NPV2])
        self.cvec = self.small("cvec_s", [128, 96])
        self.ident = self.small("ident_s", [128, 128])
        self.identb = self.small("identb_s", [128, 256], BF16)
        self.ones = self.small("ones_s", [128, 128])
        for nm, sb, dr in (("pvec", self.pvec, self.pvec_d), ("pvec2", self.pvec2, self.pvec2_d),
                           ("cvec", self.cvec, self.cvec_d), ("ident", self.ident, self.ident_d),
                           ("identb", self.identb, self.identb_d)):
            P.op("sp", lambda e, sb=sb, dr=dr: e.dma_start(out=sb[:], in_=dr), w=[nm], dma="c_" + nm)
        P.op("dve", lambda e: e.memset(self.ones[:], 1.0), w=["ones"])
        if self.stage < 90:
            P.op("dve", lambda e: e.memset(self.A[:, :], 0.0), w=["arena0"])
            P.barrier()
        P.op("dve", lambda e: e.tensor_scalar(out=self.pvec[:, PV_LNA:PV_LNA + 64], in0=self.pvec[:, 0:64],
                                              scalar1=ALPHA, scalar2=None, op0=ALU.mult),
             r=["pvec"], w=["pvec"])
        self.res = self.view(0, [KC, NT])
        self.xb = self.view(17024, [KC, NT], BF16)
        self.wgu = [self.view(25536 + 4096 * s, [2, KC, 256], BF16) for s in range(2)]
        self.wd = [self.view(33728 + 4096 * s, [4, D], BF16) for s in range(2)]
        self.actb = [self.view(41920 + 2128 * s, [4, NT], BF16) for s in range(2)]
        self.sg = [self.view(46176 + 256 * s, [512], BF16) for s in range(2)]
        self.s1 = self.view(46688, [512])
        self.mean = self.view(47200, [512])
        self.rstd = self.view(47712, [512])
        self.sqs = self.view(48224, [KC, 128])

    def load_x(self):
        res_ = self.res
        xb_ = self.xb
        P = self.P
        xT = self.din("xT", [128, KC, NT])
        for ti, (t0, n) in enumerate(TILES):
            P.op("sp", lambda e, t0=t0, n=n: e.dma_start(out=res_[:, :, t0:t0 + n], in_=xT[:, :, t0:t0 + n]),
                 w=[f"res{ti}_{dc}" for dc in range(KC)], dma=f"x{ti}")
            P.op("dve", lambda e, t0=t0, n=n: e.tensor_copy(out=xb_[:, :, t0:t0 + n], in_=res_[:, :, t0:t0 + n]),
                 r=[f"res{ti}_{dc}" for dc in range(KC)], w=[f"xb{ti}"])
            P.op("act", lambda e, t0=t0, n=n: e.activation(out=res_[:, :, t0:t0 + n], in_=res_[:, :, t0:t0 + n],
                                                           func=AF.Identity, scale=ALPHA),
                 r=[f"res{ti}_{dc}" for dc in range(KC)], w=[f"res{ti}_{dc}" for dc in range(KC)])

    def ffn(self, w_gu, w_down):
        res_ = self.res
        xb_ = self.xb
        wgu_ = self.wgu
        wd_ = self.wd
        actb_ = self.actb
        sg_ = self.sg
        P = self.P
        ps = self.ps
        groups = [(4 * g, min(4, FC - 4 * g)) for g in range((FC + 3) // 4)]
        wgu_v = w_gu.rearrange("(kc p) n -> p kc n", p=128)
        wd_v = w_down.rearrange("(c p) n -> p c n", p=128)
        halves = []
        for g, (c0, nc_) in enumerate(groups):
            for h in range(2):
                cs = c0 + 2 * h
                nch = min(2, c0 + nc_ - cs)
                if nch > 0:
                    halves.append((g, cs, nch))

        def load_gu(hi):
            g, cs, nch = halves[hi]
            slot = hi % 2
            cols = nch * 128
            for j, nm in ((0, "wg"), (1, "wu")):
                src = wgu_v[:, :, j * DFF + cs * 128: j * DFF + cs * 128 + cols]
                P.op("pool", lambda e, slot=slot, j=j, cols=cols, src=src:
                     e.dma_start(out=wgu_[slot][:, j, :, 0:cols], in_=src),
                     w=[f"{nm}{slot}"], dma=f"{nm}{slot}", nobarrier=self.weights_nobarrier)

        def load_wd(g):
            c0, nc_ = groups[g]
            slot = g % 2
            P.op("pool", lambda e, slot=slot, c0=c0, nc_=nc_:
                 e.dma_start(out=wd_[slot][:, 0:nc_, :], in_=wd_v[:, c0:c0 + nc_, :]),
                 w=[f"wd{slot}"], dma=f"wd{slot}", nobarrier=self.weights_nobarrier)

        def down(g):
            c0, nc_ = groups[g]
            ws = g % 2
            a = actb_[g % 2]
            for dc in range(KC):
                for ti, (t0, n) in enumerate(self.tiles):
                    b = 4 + (self.dcount % 2)
                    self.dcount += 1
                    for cl in range(nc_):
                        P.op("pe", lambda e, b=b, n=n, ws=ws, cl=cl, dc=dc, t0=t0, a=a, nc_=nc_:
                             e.matmul(ps[b][:, 0:n], lhsT=wd_[ws][:, cl, dc * 128:(dc + 1) * 128],
                                      rhs=a[:, cl, t0:t0 + n], start=(cl == 0), stop=(cl == nc_ - 1)),
                             r=[f"wd{ws}", f"act{g % 2}_{cl}_{ti}"], w=[f"ps{b}"])
                    P.op("dve", lambda e, b=b, n=n, dc=dc, t0=t0:
                         e.scalar_tensor_tensor(out=res_[:, dc, t0:t0 + n], in0=ps[b][:, 0:n], scalar=0.5,
                                                in1=res_[:, dc, t0:t0 + n], op0=ALU.mult, op1=ALU.add),
                         r=[f"ps{b}", f"res{ti}_{dc}"], w=[f"res{ti}_{dc}"])

        self.dcount = 0
        it = 0
        load_gu(0)
        load_gu(1)
        load_wd(0)
        for hi, (g, cs, nch) in enumerate(halves):
            slot = hi % 2
            for cc in range(nch):
                cl = cs + cc - groups[g][0]
                for ti, (t0, n) in enumerate(self.tiles):
                    bg = 2 * (it % 2)
                    bu = bg + 1
                    st = it % 2
                    it += 1
                    for j, b, nm in ((0, bg, "wg"), (1, bu, "wu")):
                        for kc in range(KC):
                            P.op("pe", lambda e, b=b, n=n, slot=slot, j=j, kc=kc, cc=cc, t0=t0:
                                 e.matmul(ps[b][:, 0:n], lhsT=wgu_[slot][:, j, kc, cc * 128:(cc + 1) * 128],
                                          rhs=xb_[:, kc, t0:t0 + n], start=(kc == 0), stop=(kc == KC - 1)),
                                 r=[f"{nm}{slot}", f"xb{ti}"], w=[f"ps{b}"])
                    P.op("act", lambda e, bg=bg, n=n, st=st:
                         e.activation(out=sg_[st][:, 0:n], in_=ps[bg][:, 0:n], func=AF.Silu),
                         r=[f"ps{bg}"], w=[f"sg{st}"])
                    P.op("dve", lambda e, bu=bu, n=n, st=st, g=g, cl=cl, t0=t0:
                         e.tensor_tensor(out=actb_[g % 2][:, cl, t0:t0 + n], in0=sg_[st][:, 0:n],
                                         in1=ps[bu][:, 0:n], op=ALU.mult),
                         r=[f"sg{st}", f"ps{bu}"], w=[f"act{g % 2}_{cl}_{ti}"])
            if hi + 2 < len(halves):
                load_gu(hi + 2)
            last_half_of_group = (hi + 1 == len(halves)) or (halves[hi + 1][0] != g)
            if last_half_of_group:
                if g >= 1:
                    down(g - 1)
                if g + 1 < len(groups):
                    load_wd(g + 1)
        down(len(groups) - 1)

    def layer_norm(self, gcol, bcol, final=False):
        res_ = self.res
        xb_ = self.xb
        s1_ = self.s1
        mean_ = self.mean
        rstd_ = self.rstd
        sqs_ = self.sqs
        P = self.P
        ps = self.ps
        pv = self.pvec
        for ti, (t0, n) in enumerate(self.tiles):
            rt = [f"res{ti}_{dc}" for dc in range(KC)]
            resT = res_[:, :, t0:t0 + n]
            P.op("dve", lambda e, resT=resT, n=n: e.tensor_reduce(out=s1_[:, 0:n], in_=resT.rearrange("p c n -> p n c"),
                                                                  axis=AX.X, op=ALU.add), r=rt, w=["s1"])
            P.op("pe", lambda e, n=n: e.matmul(ps[6][:, 0:n], lhsT=self.ones[:], rhs=s1_[:, 0:n], start=True, stop=True),
                 r=["s1", "ones"], w=["ps6"])
            P.op("act", lambda e, n=n: e.activation(out=mean_[:, 0:n], in_=ps[6][:, 0:n], func=AF.Identity, scale=1.0 / D),
                 r=["ps6"], w=["mean"])
            P.op("dve", lambda e, resT=resT, n=n: e.tensor_tensor(
                out=resT, in0=resT, in1=mean_[:, 0:n].unsqueeze(1).broadcast_to([128, KC, n]), op=ALU.subtract),
                r=rt + ["mean"], w=rt)
            for j0 in range(0, n, 128):
                m = min(128, n - j0)
                P.op("act", lambda e, t0=t0, j0=j0, m=m: e.activation(out=sqs_[:, :, 0:m], in_=res_[:, :, t0 + j0:t0 + j0 + m],
                                                                      func=AF.Square), r=rt, w=["sqs"])
                P.op("dve", lambda e, j0=j0, m=m: e.tensor_reduce(out=s1_[:, j0:j0 + m], in_=sqs_[:, :, 0:m].rearrange("p c n -> p n c"),
                                                                  axis=AX.X, op=ALU.add), r=["sqs"], w=["s1"])
            P.op("pe", lambda e, n=n: e.matmul(ps[7][:, 0:n], lhsT=self.ones[:], rhs=s1_[:, 0:n], start=True, stop=True),
                 r=["s1", "ones"], w=["ps7"])
            P.op("dve", lambda e, n=n: e.tensor_scalar(out=rstd_[:, 0:n], in0=ps[7][:, 0:n], scalar1=1.0 / D, scalar2=LN_EPS,
                                                       op0=ALU.mult, op1=ALU.add), r=["ps7"], w=["rstd"])
            P.op("act", lambda e, n=n: e.activation(out=rstd_[:, 0:n], in_=rstd_[:, 0:n], func=AF.Sqrt), r=["rstd"], w=["rstd"])
            P.op("dve", lambda e, n=n: e.reciprocal(out=rstd_[:, 0:n], in_=rstd_[:, 0:n]), r=["rstd"], w=["rstd"])
            P.op("dve", lambda e, resT=resT, n=n: e.tensor_tensor(
                out=resT, in0=resT, in1=rstd_[:, 0:n].unsqueeze(1).broadcast_to([128, KC, n]), op=ALU.mult),
                r=rt + ["rstd"], w=rt)
            for dc in range(KC):
                rdc = res_[:, dc, t0:t0 + n]
                if not final:
                    P.op("act", lambda e, rdc=rdc, dc=dc, t0=t0, n=n: e.activation(
                        out=xb_[:, dc, t0:t0 + n], in_=rdc, func=AF.Identity,
                        scale=pv[:, gcol + dc:gcol + dc + 1], bias=pv[:, bcol + dc:bcol + dc + 1]),
                        r=[f"res{ti}_{dc}", "pvec"], w=[f"xb{ti}"])
                    ga, ba = PV_LNA + gcol + dc, PV_LNA + bcol + dc
                else:
                    ga, ba = gcol + dc, bcol + dc
                P.op("act", lambda e, rdc=rdc, ga=ga, ba=ba: e.activation(
                    out=rdc, in_=rdc, func=AF.Identity, scale=pv[:, ga:ga + 1], bias=pv[:, ba:ba + 1]),
                    r=[f"res{ti}_{dc}", "pvec"], w=[f"res{ti}_{dc}"])

    def phase_b(self, w_in):
        res_ = self.res
        xb_ = self.xb
        P = self.P
        ps = self.ps
        nc = self.nc
        pv2 = self.pvec2
        self.res_spill = nc.dram_tensor("res_spill", [128, KC, NT], F32, kind="Internal").ap()
        self.GR = [320, 256, 258, 258]
        self.g_in = [nc.dram_tensor(f"g{i}_in", [r, 1024], BF16, kind="Internal").ap() for i, r in enumerate(self.GR)]
        self.g_out = [nc.dram_tensor(f"g{i}_out", [4 * r, 1024], BF16, kind="Internal").ap() for i, r in enumerate(self.GR)]
        self.c_in = nc.dram_tensor("c_in", [128, 128], F32, kind="Internal").ap()
        self.c_out = nc.dram_tensor("c_out", [512, 128], F32, kind="Internal").ap()
        self.k_out = self.dout("k_out", [128, 4, NPB + NS])
        self.v_out = self.dout("v_out", [NPB + NS, 512])
        self.ki_out = self.dout("ki_out", [64, NPB + NS])
        self.st_out = self.dout("st_out", [128, 8, 20])
        lwa_d = self.din("lru_w_a", [8, 128, 128])
        lwi_d = self.din("lru_w_i", [8, 128, 128])
        sconv_d = self.din("sconvT", [128, 8, 4, 3])
        srnn_d = self.din("srnnT", [128, 8, 4])
        sel_d = self.din("selT", [128, 8, 32])

        for ti, (t0, n) in enumerate(TILES):
            P.op("sp", lambda e, t0=t0, n=n: e.dma_start(out=self.res_spill[:, :, t0:t0 + n], in_=res_[:, :, t0:t0 + n]),
                 r=[f"res{ti}_{dc}" for dc in range(KC)], w=["spill"], dma=f"spill{ti}")
        P.barrier()

        self.mixT = self.view(0, [KC, NPB + NS], BF16)
        self.PG = self.view(8320, [8, NPB + NS], BF16)
        self.qT = self.view(12480, [8, NPB + NS], BF16)
        win = [self.view(25536 + 4096 * s, [KC, 512], BF16) for s in range(2)]
        self.qiT = self.view(33728, [8, NPB + NS], BF16, parts=64)
        T = [self.view(37888 + NT * i, [NT]) for i in range(5)]
        xcb = self.view(43208, [NPB + NS], BF16)
        stg = [self.view(46688 + 512 * s, [512]) for s in range(2)]
        stgb = [self.view(47712 + 260 * s, [4, 129], BF16) for s in range(2)]
        stgk = [self.view(48232 + 256 * s, [512], BF16) for s in range(2)]
        lwa = self.view(50272, [8, 128], BF16)
        lwi = self.view(50784, [8, 128], BF16)
        self.wtok = self.small("wtok", [128, 9, 8])
        self.kts = self.small("kts", [128, 4, NS], BF16)
        self.vs = self.view(43728, [4, 129], BF16, parts=NS)
        self.kis = self.small("kis", [64, NS], BF16)
        sconv = self.view(43988, [8, 4, 3])
        srnn = self.view(44084, [8, 4])
        self.sel = self.view(44116, [8, 32])
        stt = self.view(44372, [8, 20])
        chain = self.view(44532, [8, 8, 2])
        zeros = self.view(44660, [128])
        self.chain, self.stt = chain, stt
        for nm, sb, dr in (("sconv", sconv, sconv_d), ("srnn", srnn, srnn_d), ("sel", self.sel, sel_d)):
            P.op("sp", lambda e, sb=sb, dr=dr: e.dma_start(out=sb, in_=dr), w=[nm], dma="c_" + nm)
        P.op("pool", lambda e: e.dma_start(out=lwa, in_=lwa_d.rearrange("n c d -> c n d")), w=["lwa"], dma="c_lwa")
        P.op("pool", lambda e: e.dma_start(out=lwi, in_=lwi_d.rearrange("n c d -> c n d")), w=["lwi"], dma="c_lwi")
        P.op("dve", lambda e: e.memset(zeros, 0.0), w=["zeros"])
        for s in range(2):
            P.op("dve", lambda e, s=s: e.memset(stgb[s], 1.0), w=[f"stgb{s}"])
        P.op("dve", lambda e: e.memset(self.vs, 1.0), w=["vs"])
        P.op("act", lambda e: e.activation(out=pv2[:, PV2_SP:PV2_SP + 8], in_=pv2[:, PV2_LAM:PV2_LAM + 8], func=AF.Exp, scale=-1.0),
             r=["pvec2"], w=["pvec2"])
        P.op("act", lambda e: e.activation(out=pv2[:, PV2_SP:PV2_SP + 8], in_=pv2[:, PV2_SP:PV2_SP + 8], func=AF.Ln, bias=1.0),
             r=["pvec2"], w=["pvec2"])
        P.op("dve", lambda e: e.tensor_scalar(out=pv2[:, PV2_SP:PV2_SP + 8], in0=pv2[:, PV2_SP:PV2_SP + 8], scalar1=-8.0,
                                              scalar2=None, op0=ALU.mult), r=["pvec2"], w=["pvec2"])

        w_v = w_in.rearrange("(kc p) n -> p kc n", p=128)
        self.wcount = 0

        def load_w(c0, ncols):
            slot = self.wcount % 2
            self.wcount += 1
            P.op("pool", lambda e, slot=slot, c0=c0, ncols=ncols:
                 e.dma_start(out=win[slot][:, :, 0:ncols], in_=w_v[:, :, c0:c0 + ncols]),
                 w=[f"win{slot}"], dma=f"win{slot}", nobarrier=True)
            return slot

        self.bcount = 0

        def bank():
            b = self.bcount % 6
            self.bcount += 1
            return b

        xbt = lambda t0: ["xb0", "xb1", "xb2"][0 if t0 < 512 else (1 if t0 < 1024 else 2)]
        FT = [(0, 512, 0), (512, 512, 512), (SMP0, NS, NPB)]
        order = [("k", 3072, 512), ("v", 3584, 512), ("kiwi", 4608, 72), ("qi", 4096, 512),
                 ("q0", 2048, 512), ("q1", 2560, 512), ("xr0", 0, 512), ("gr0", 1024, 512),
                 ("xr1", 512, 512), ("gr1", 1536, 512)]
        slots = {}
        slots[order[0][0]] = load_w(order[0][1], order[0][2])
        slots[order[1][0]] = load_w(order[1][1], order[1][2])
        self.scount = 0

        def fm_proj(slot, col, M, evac):
            for (s0, n, d0) in FT:
                b = bank()
                for kc in range(KC):
                    P.op("pe", lambda e, b=b, n=n, slot=slot, kc=kc, col=col, M=M, s0=s0:
                         e.matmul(ps[b][0:M, 0:n], lhsT=win[slot][:, kc, col:col + M], rhs=xb_[:, kc, s0:s0 + n],
                                  start=(kc == 0), stop=(kc == KC - 1)),
                         r=[f"win{slot}", xbt(s0)], w=[f"ps{b}"])
                evac(b, n, d0)

        for oi, (nm, c0, ncols) in enumerate(order):
            if self.upto is not None and nm == self.upto:
                break
            slot = slots[nm]
            if nm == "k":
                for n_ in range(4):
                    def evac(b, n, d0, n_=n_):
                        st = self.scount % 2
                        self.scount += 1
                        P.op("dve", lambda e: e.tensor_copy(out=stg[st][:, 0:n], in_=ps[b][:, 0:n]), r=[f"ps{b}"], w=[f"stg{st}"])
                        P.op("sp", lambda e: e.dma_start(out=self.k_out[:, n_, d0:d0 + n], in_=stg[st][:, 0:n]),
                             r=[f"stg{st}"], w=[self.otok("k")], dma=f"stg{st}")
                        import os
                        if d0 < NPB:
                            if os.environ.get("DBG_SKIP") == "gin":
                                return
                            P.op("act", lambda e: e.activation(out=stgk[st][:, 0:n], in_=ps[b][:, 0:n], func=AF.Identity),
                                 r=[f"ps{b}"] + ([f"stg{st}"] if os.environ.get("DBG_SER") else []), w=[f"stgk{st}"])
                            if os.environ.get("DBG_SKIP") == "gindma":
                                return
                            P.op("sp", lambda e: e.dma_start(out=self.g_in[n_ // 2][(n_ % 2) * 128:(n_ % 2 + 1) * 128, d0:d0 + n], in_=stgk[st][:, 0:n]),
                                 r=[f"stgk{st}"], w=[self.gtok()], dma=f"stgk{st}")
                        else:
                            if os.environ.get("DBG_SKIP") in ("gin", "kts"):
                                return
                            P.op("act", lambda e: e.activation(out=self.kts[:, n_, :], in_=ps[b][:, 0:n], func=AF.Identity),
                                 r=[f"ps{b}"], w=["kts"])
                    fm_proj(slot, n_ * 128, 128, evac)
            elif nm == "v":
                blocks = [(s * 128, 128, s * 128) for s in range(8)] + [(SMP0, NS, NPB)]
                for (s0, m, d0) in blocks:
                    b = bank()
                    for kc in range(KC):
                        P.op("pe", lambda e, b=b, m=m, kc=kc, s0=s0, slot=slot:
                             e.matmul(ps[b][0:m, 0:512], lhsT=xb_[:, kc, s0:s0 + m], rhs=win[slot][:, kc, 0:512],
                                      start=(kc == 0), stop=(kc == KC - 1)),
                             r=[f"win{slot}", xbt(s0)], w=[f"ps{b}"])
                    st = self.scount % 2
                    self.scount += 1
                    P.op("dve", lambda e, b=b, m=m, st=st: e.tensor_copy(out=stg[st][0:m, :], in_=ps[b][0:m, :]),
                         r=[f"ps{b}"], w=[f"stg{st}"])
                    P.op("sp", lambda e, m=m, st=st, d0=d0: e.dma_start(out=self.v_out[d0:d0 + m, :], in_=stg[st][0:m, :]),
                         r=[f"stg{st}"], w=[self.otok("v")], dma=f"stg{st}")
                    if d0 < NPB:
                        P.op("act", lambda e, b=b, st=st: e.activation(
                            out=stgb[st][:, :, 0:128], in_=ps[b][:, :].rearrange("p (a b) -> p a b", a=4), func=AF.Identity),
                            r=[f"ps{b}"], w=[f"stgb{st}"])
                        dd = d0 % 512
                        dst = self.g_in[2 + d0 // 512].rearrange("r c -> (r c)")[dd * 516:(dd + 128) * 516].rearrange("(t c) -> t c", c=516)
                        P.op("sp", lambda e, st=st, dst=dst: e.dma_start(out=dst, in_=stgb[st].rearrange("p a b -> p (a b)")),
                             r=[f"stgb{st}"], w=[self.gtok()], dma=f"stgb{st}")
                    else:
                        P.op("act", lambda e, b=b, m=m: e.activation(
                            out=self.vs[:, :, 0:128], in_=ps[b][0:m, :].rearrange("p (a b) -> p a b", a=4), func=AF.Identity),
                            r=[f"ps{b}"], w=["vs"])
            elif nm == "kiwi":
                def evac(b, n, d0):
                    st = self.scount % 2
                    self.scount += 1
                    P.op("dve", lambda e: e.tensor_copy(out=stg[st][0:64, 0:n], in_=ps[b][0:64, 0:n]), r=[f"ps{b}"], w=[f"stg{st}"])
                    P.op("sp", lambda e: e.dma_start(out=self.ki_out[:, d0:d0 + n], in_=stg[st][0:64, 0:n]),
                         r=[f"stg{st}"], w=[self.otok("ki")], dma=f"stg{st}")
                    if d0 < NPB:
                        P.op("act", lambda e: e.activation(out=stgk[st][0:64, 0:n], in_=ps[b][0:64, 0:n], func=AF.Identity),
                             r=[f"ps{b}"], w=[f"stgk{st}"])
                        P.op("sp", lambda e: e.dma_start(out=self.g_in[0][256:320, d0:d0 + n], in_=stgk[st][0:64, 0:n]),
                             r=[f"stgk{st}"], w=[self.gtok()], dma=f"stgk{st}")
                    else:
                        P.op("act", lambda e: e.activation(out=self.kis[:, :], in_=ps[b][0:64, 0:n], func=AF.Identity),
                             r=[f"ps{b}"], w=["kis"])
                fm_proj(slot, 0, 64, evac)
                blocks = [(s * 128, 128) for s in range(8)] + [(SMP0, NS)]
                b = bank()
                for bi, (s0, m) in enumerate(blocks):
                    for kc in range(KC):
                        P.op("pe", lambda e, b=b, m=m, kc=kc, s0=s0, bi=bi, slot=slot:
                             e.matmul(ps[b][0:m, bi * 8:bi * 8 + 8], lhsT=xb_[:, kc, s0:s0 + m], rhs=win[slot][:, kc, 64:72],
                                      start=(kc == 0), stop=(kc == KC - 1)),
                             r=[f"win{slot}", xbt(s0)], w=[f"ps{b}"])
                P.op("dve", lambda e, b=b: e.tensor_copy(out=self.wtok[:], in_=ps[b][:, 0:72].rearrange("p (a b) -> p a b", a=9)),
                     r=[f"ps{b}"], w=["wtok"])
            elif nm == "qi":
                for h in range(8):
                    def evac(b, n, d0, h=h):
                        P.op("act", lambda e: e.activation(out=self.qiT[:, h, d0:d0 + n], in_=ps[b][0:64, 0:n], func=AF.Identity),
                             r=[f"ps{b}"], w=["qiT"])
                    fm_proj(slot, h * 64, 64, evac)
            elif nm in ("q0", "q1"):
                for hh in range(4):
                    h = hh + (4 if nm == "q1" else 0)
                    def evac(b, n, d0, h=h):
                        P.op("act", lambda e: e.activation(out=self.qT[:, h, d0:d0 + n], in_=ps[b][:, 0:n], func=AF.Identity),
                             r=[f"ps{b}"], w=["qT"])
                    fm_proj(slot, hh * 128, 128, evac)
            elif nm in ("xr0", "xr1"):
                gslot = slots[order[oi + 1][0]] if order[oi + 1][0] in slots else None
                if gslot is None:
                    gslot = slots[order[oi + 1][0]] = load_w(order[oi + 1][1], order[oi + 1][2])
                for cc in range(4):
                    ct = cc + (4 if nm == "xr1" else 0)
                    self.rnn_tile(ct, cc, slot, gslot, win, T, xcb, lwa, lwi, sconv, srnn, zeros, xbt)
            if oi + 2 < len(order) and order[oi + 2][0] not in slots:
                if not nm.startswith("xr"):
                    slots[order[oi + 2][0]] = load_w(order[oi + 2][1], order[oi + 2][2])
            if nm.startswith("gr") and oi + 1 < len(order):
                nxt = order[oi + 1][0]
                if nxt not in slots:
                    slots[nxt] = load_w(order[oi + 1][1], order[oi + 1][2])

        if self.upto is not None and self.upto != 'ccB':
            if self.upto != 'cc':
                return
        for gi in range(4):
            P.op("pool", lambda e, gi=gi: e.collective_compute("AllGather", ALU.bypass, replica_groups=[[0, 1, 2, 3], [4, 5, 6, 7]],
                                                               ins=[self.g_in[gi]], outs=[self.g_out[gi]]),
                 r=list(self.gtoks), w=[f"g_out{gi}"], dma="ccA", cc=True)
        P.op("sp", lambda e: e.dma_start(out=self.c_in, in_=chain.rearrange("p a b c -> p (a b c)")), r=["chain"], w=["c_in"], dma="c_in")
        P.op("pool", lambda e: e.collective_compute("AllGather", ALU.bypass, replica_groups=[[0, 1, 2, 3], [4, 5, 6, 7]],
                                                    ins=[self.c_in], outs=[self.c_out]),
             r=["c_in"], w=["c_out"], dma="ccB", cc=True)

    def rnn_tile(self, ct, cc, xslot, gslot, win, T, xcb, lwa, lwi, sconv, srnn, zeros, xbt):
        xb_ = self.xb
        P = self.P
        ps = self.ps
        pv2 = self.pvec2
        cv = self.cvec
        NQ = NPB + NS
        T0, T1, T2, T3, T4 = T
        tk = [f"T{i}" for i in range(5)]
        for (t0, n) in TILES:
            b = self.bcount % 6
            self.bcount += 1
            for kc in range(KC):
                P.op("pe", lambda e, b=b, n=n, kc=kc, t0=t0:
                     e.matmul(ps[b][:, 0:n], lhsT=win[xslot][:, kc, cc * 128:(cc + 1) * 128], rhs=xb_[:, kc, t0:t0 + n],
                              start=(kc == 0), stop=(kc == KC - 1)),
                     r=[f"win{xslot}", xbt(t0)], w=[f"ps{b}"])
            P.op("act", lambda e, b=b, n=n, t0=t0: e.activation(out=T0[:, t0:t0 + n], in_=ps[b][:, 0:n], func=AF.Identity),
                 r=[f"ps{b}"], w=[tk[0]])
        P.op("dve", lambda e: e.tensor_copy(out=self.stt[:, ct, 0:3], in_=T0[:, NPB - 3:NPB]), r=[tk[0]], w=["stt"])
        P.op("dve", lambda e: e.tensor_copy(out=self.stt[:, ct, 3:15].rearrange("p (a b) -> p a b", a=4),
                                            in_=T0[:, SMP0:SMP0 + NS].rearrange("p (a b) -> p a b", a=4)[:, :, 1:4]),
             r=[tk[0]], w=["stt"])
        xp = T1[:, 0:1048].rearrange("p (a b) -> p a b", a=8)
        xps = T3[:, 0:28].rearrange("p (a b) -> p a b", a=4)
        P.op("dve", lambda e: e.tensor_tensor(out=xp[:, :, 0:3], in0=T0[:, HALO0:HALO0 + NH].rearrange("p (a b) -> p a b", a=8),
                                              in1=cv[:, 8:8 + NH].rearrange("p (a b) -> p a b", a=8), op=ALU.mult),
             r=[tk[0], "cvec"], w=[tk[1]])
        P.op("dve", lambda e: e.tensor_copy(out=xp[:, :, 3:131], in_=T0[:, 0:NPB].rearrange("p (a b) -> p a b", a=8)),
             r=[tk[0]], w=[tk[1]])
        P.op("dve", lambda e: e.tensor_copy(out=xps[:, :, 0:3], in_=sconv[:, ct, :, :]), r=["sconv"], w=[tk[3]])
        P.op("dve", lambda e: e.tensor_copy(out=xps[:, :, 3:7], in_=T0[:, SMP0:SMP0 + NS].rearrange("p (a b) -> p a b", a=4)),
             r=[tk[0]], w=[tk[3]])
        xc_p = T2[:, 0:NPB].rearrange("p (a b) -> p a b", a=8)
        xc_s = T2[:, NPB:NQ].rearrange("p (a b) -> p a b", a=4)
        wcol = lambda j: pv2[:, PV2_CONVW + ct * 4 + j:PV2_CONVW + ct * 4 + j + 1]
        bcol = pv2[:, PV2_CONVB + ct:PV2_CONVB + ct + 1]
        for (xc_, xp_, L, rk) in ((xc_p, xp, 128, tk[1]), (xc_s, xps, 4, tk[3])):
            P.op("act", lambda e, xc_=xc_, xp_=xp_, L=L: e.activation(out=xc_, in_=xp_[:, :, 0:L], func=AF.Identity,
                                                                     scale=wcol(0), bias=bcol),
                 r=[rk, "pvec2"], w=[tk[2]])
            for j in range(1, 4):
                P.op("dve", lambda e, xc_=xc_, xp_=xp_, L=L, j=j: e.scalar_tensor_tensor(
                    out=xc_, in0=xp_[:, :, j:j + L], scalar=wcol(j), in1=xc_, op0=ALU.mult, op1=ALU.add),
                    r=[rk, tk[2], "pvec2"], w=[tk[2]])
        P.op("dve", lambda e: e.tensor_copy(out=xcb[:, :], in_=T2[:, 0:NQ]), r=[tk[2]], w=["xcb"])
        for (lw, lwn, bc0, dst, dk) in ((lwa, "lwa", PV2_BA, T3, tk[3]), (lwi, "lwi", PV2_BI, T4, tk[4])):
            for (t0, n) in ((0, 512), (512, 512), (NPB, NS)):
                b = 6 + (self.gcount % 2)
                self.gcount += 1
                P.op("pe", lambda e, b=b, n=n, t0=t0, lw=lw: e.matmul(ps[b][:, 0:n], lhsT=lw[:, ct, :], rhs=xcb[:, t0:t0 + n],
                                                                      start=True, stop=True),
                     r=[lwn, "xcb"], w=[f"ps{b}"])
                P.op("act", lambda e, b=b, n=n, t0=t0, dst=dst, bc0=bc0: e.activation(
                    out=dst[:, t0:t0 + n], in_=ps[b][:, 0:n], func=AF.Sigmoid, bias=pv2[:, bc0 + ct:bc0 + ct + 1]),
                    r=[f"ps{b}", "pvec2"], w=[dk])
        P.op("act", lambda e: e.activation(out=T3[:, 0:NQ], in_=T3[:, 0:NQ], func=AF.Exp, scale=pv2[:, PV2_SP + ct:PV2_SP + ct + 1]),
             r=[tk[3], "pvec2"], w=[tk[3]])
        P.op("dve", lambda e: e.tensor_tensor(out=T0[:, 0:NQ], in0=T3[:, 0:NQ], in1=T3[:, 0:NQ], op=ALU.mult), r=[tk[3]], w=[tk[0]])
        P.op("act", lambda e: e.activation(out=T0[:, 0:NQ], in_=T0[:, 0:NQ], func=AF.Sqrt, scale=-1.0, bias=1.0), r=[tk[0]], w=[tk[0]])
        P.op("dve", lambda e: e.tensor_tensor(out=T4[:, 0:NQ], in0=T4[:, 0:NQ], in1=T2[:, 0:NQ], op=ALU.mult), r=[tk[4], tk[2]], w=[tk[4]])
        P.op("dve", lambda e: e.tensor_tensor(out=T4[:, 0:NQ], in0=T4[:, 0:NQ], in1=T0[:, 0:NQ], op=ALU.mult), r=[tk[4], tk[0]], w=[tk[4]])
        for s in range(8):
            sl = slice(s * 128, (s + 1) * 128)
            P.op("dve", lambda e, sl=sl: e.tensor_tensor_scan(out=T0[:, sl], data0=T3[:, sl], data1=T4[:, sl], initial=0.0,
                                                              op0=ALU.mult, op1=ALU.add), r=[tk[3], tk[4]], w=[tk[0]])
            P.op("dve", lambda e, sl=sl: e.tensor_tensor_scan(out=T2[:, sl], data0=T3[:, sl], data1=zeros, initial=1.0,
                                                              op0=ALU.mult, op1=ALU.add), r=[tk[3], "zeros"], w=[tk[2]])
        for bq in range(4):
            sl = slice(NPB + 4 * bq, NPB + 4 * bq + 4)
            P.op("dve", lambda e, sl=sl, bq=bq: e.tensor_tensor_scan(out=T0[:, sl], data0=T3[:, sl], data1=T4[:, sl],
                                                                     initial=srnn[:, ct, bq:bq + 1], op0=ALU.mult, op1=ALU.add),
                 r=[tk[3], tk[4], "srnn"], w=[tk[0]])
        P.op("dve", lambda e: e.tensor_copy(out=self.chain[:, ct, :, 0], in_=T2[:, 0:NPB].rearrange("p (a b) -> p a b", a=8)[:, :, 127]),
             r=[tk[2]], w=["chain"])
        P.op("dve", lambda e: e.tensor_copy(out=self.chain[:, ct, :, 1], in_=T0[:, 0:NPB].rearrange("p (a b) -> p a b", a=8)[:, :, 127]),
             r=[tk[0]], w=["chain"])
        P.op("dve", lambda e: e.tensor_copy(out=self.stt[:, ct, 16:20], in_=T0[:, NPB:NQ].rearrange("p (a b) -> p a b", a=4)[:, :, 3]),
             r=[tk[0]], w=["stt"])
        for (s0, n, d0) in ((0, 512, 0), (512, 512, 512), (SMP0, NS, NPB)):
            b = self.bcount % 6
            self.bcount += 1
            for kc in range(KC):
                P.op("pe", lambda e, b=b, n=n, kc=kc, s0=s0:
                     e.matmul(ps[b][:, 0:n], lhsT=win[gslot][:, kc, cc * 128:(cc + 1) * 128], rhs=xb_[:, kc, s0:s0 + n],
                              start=(kc == 0), stop=(kc == KC - 1)),
                     r=[f"win{gslot}", xbt(s0)], w=[f"ps{b}"])
            P.op("act", lambda e, b=b, n=n, d0=d0: e.activation(out=T1[:, d0:d0 + n], in_=ps[b][:, 0:n], func=AF.Identity),
                 r=[f"ps{b}"], w=[tk[1]])
        X = T1[:, 0:NQ]
        G = T3[:, 0:NQ]
        P.op("dve", lambda e: e.tensor_tensor(out=G, in0=X, in1=X, op=ALU.mult), r=[tk[1]], w=[tk[3]])
        P.op("dve", lambda e: e.tensor_scalar(out=G, in0=G, scalar1=0.044715, scalar2=1.0, op0=ALU.mult, op1=ALU.add), r=[tk[3]], w=[tk[3]])
        P.op("dve", lambda e: e.tensor_tensor(out=G, in0=G, in1=X, op=ALU.mult), r=[tk[3], tk[1]], w=[tk[3]])
        P.op("act", lambda e: e.activation(out=G, in_=G, func=AF.Sigmoid, scale=1.5957691216057308), r=[tk[3]], w=[tk[3]])
        P.op("dve", lambda e: e.tensor_tensor(out=G, in0=G, in1=X, op=ALU.mult), r=[tk[3], tk[1]], w=[tk[3]])
        P.op("dve", lambda e: e.tensor_tensor(out=self.mixT[:, ct, :], in0=T0[:, 0:NQ], in1=G, op=ALU.mult), r=[tk[0], tk[3]], w=[f"mix{ct}"])
        P.op("dve", lambda e: e.tensor_tensor(out=self.PG[:, ct, 0:NPB], in0=T2[:, 0:NPB], in1=T3[:, 0:NPB], op=ALU.mult),
             r=[tk[2], tk[3]], w=[f"pg{ct}"])

    def phase_c(self):
        P = self.P
        ps = self.ps
        nc = self.nc
        NQ = NPB + NS
        P.barrier()
        KT = self.view(17024, [4, 8, 4, 128], BF16)
        VA = self.view(25216, [8, 4, 516], BF16)
        kiT = self.view(46688, [8, 4, 128], BF16, parts=64)
        sc = self.view(37888, [4096])
        Rb = [self.view(41984 + 256 * j, [512], BF16) for j in range(2)]
        PTb = [self.view(42496 + 256 * j, [512], BF16) for j in range(2)]
        NM = self.view(48736, [4096], BF16)
        cb = self.view(50784, [512])
        kpi = self.view(51296, [512], I32)
        DG = self.view(44788, [8, 128], BF16)
        ao = self.view(45300, [8, 128], BF16)
        wabs = self.small("wabs", [128, 72])
        wsgn = self.small("wsgn", [128, 72])
        sm = self.small("sm", [128, 64])
        hw = self.small("hwt", [128, 24])
        self.KT, self.VA, self.kiT = KT, VA, kiT
        cvec = self.cvec
        C_IDX = (64.0 ** -0.5) * (8.0 ** -0.5)
        ATT = 128.0 ** -0.5
        NIT = 20

        for r in range(4):
            for n in range(4):
                gi, R_ = n // 2, self.GR[n // 2]
                src = self.g_out[gi][r * R_ + (n % 2) * 128:r * R_ + (n % 2) * 128 + 128, :].rearrange("d (s i) -> d s i", i=128)
                P.op("sp", lambda e, r=r, n=n, src=src: e.dma_start(out=KT[:, n, :, r, :], in_=src),
                     r=[f"g_out{gi}"], w=[f"KT{r}{n}"], dma="ldk")
            src = self.g_out[0][r * 320 + 256:r * 320 + 320, :].rearrange("d (s i) -> d s i", i=128)
            P.op("sp", lambda e, r=r, src=src: e.dma_start(out=kiT[:, :, r, :], in_=src), r=["g_out0"], w=[f"kiT{r}"], dma="ldk")
            for hf in range(2):
                src = self.g_out[2 + hf][r * 258:(r + 1) * 258, :].rearrange("r c -> (r c)").rearrange("(s i c) -> i s c", s=4, i=128)
                P.op("sp", lambda e, r=r, hf=hf, src=src: e.dma_start(out=VA[:, 4 * hf:4 * hf + 4, r, :], in_=src),
                     r=[f"g_out{2 + hf}"], w=[f"VA{r}{hf}"], dma="ldk")
        ktok = [f"KT{r}{n}" for r in range(4) for n in range(4)]
        kitok = [f"kiT{r}" for r in range(4)]
        vtok = [f"VA{r}{hf}" for r in range(4) for hf in range(2)]

        self.rnn_fixup()
        wt = self.wtok[:].rearrange("p a b -> p (a b)")
        P.op("act", lambda e: e.activation(out=wsgn[:], in_=wt, func=AF.Sign), r=["wtok"], w=["wsgn"])
        P.op("dve", lambda e: e.scalar_tensor_tensor(out=wabs[:], in0=wt, scalar=C_IDX, in1=wsgn[:], op0=ALU.mult, op1=ALU.mult),
             r=["wtok", "wsgn"], w=["wabs"])

        for s in range(8):
            nkc = s + 1
            Ls = 512 * nkc
            qs = slice(s * 128, (s + 1) * 128)
            for h in range(8):
                P.op("pool", lambda e, h=h, s=s: e.tensor_scalar(out=DG[:, h, :], in0=self.identb[:, 0:128],
                                                                 scalar1=wsgn[:, s * 8 + h:s * 8 + h + 1], scalar2=None, op0=ALU.mult),
                     r=["identb", "wsgn"], w=[f"DG{h}"])
            for kc in range(nkc):
                for h in range(8):
                    b = h % 2
                    P.op("pe", lambda e, b=b, h=h, kc=kc, qs=qs: e.matmul(
                        ps[b][:, :], lhsT=self.qiT[:, h, qs], rhs=kiT[:, kc, :, :].rearrange("p a b -> p (a b)"), start=True, stop=True),
                        r=["qiT"] + ktok + kitok + vtok, w=[f"ps{b}"])
                    P.op("act", lambda e, b=b, h=h, s=s: e.activation(out=Rb[b], in_=ps[b][:, :], func=AF.Relu,
                                                                      scale=wabs[:, s * 8 + h:s * 8 + h + 1]),
                         r=[f"ps{b}", "wabs"], w=[f"Rb{b}"])
                    P.op("pe", lambda e, b=b, h=h: e.matmul(ps[2][:, :], lhsT=DG[:, h, :], rhs=Rb[b], start=(h == 0), stop=(h == 7)),
                         r=[f"Rb{b}", f"DG{h}"], w=["ps2"])
                P.op("dve", lambda e, kc=kc: e.tensor_reduce(out=sm[:, 32 + kc:33 + kc], in_=ps[2][:, :], axis=AX.X, op=ALU.max,
                                                             apply_absolute_value=True), r=["ps2"], w=["mx"])
                if kc < s:
                    P.op("dve", lambda e, kc=kc: e.tensor_copy(out=sc[:, kc * 512:(kc + 1) * 512], in_=ps[2][:, :]), r=["ps2"], w=["sc"])
                else:
                    base = 128 * (8 * (s // 2) + (0 if s % 2 == 0 else 7))
                    step = 128 if s % 2 == 0 else -128
                    P.op("pool", lambda e, base=base, step=step: e.iota(kpi, pattern=[[step, 4], [1, 128]], base=base, channel_multiplier=0),
                         w=["kpi"])
                    P.op("dve", lambda e: e.tensor_copy(out=cb, in_=kpi), r=["kpi"], w=["cb"])
                    P.op("dve", lambda e, s=s: e.tensor_scalar(out=cb, in0=cb, scalar1=cvec[:, s:s + 1], scalar2=NEG, op0=ALU.is_gt, op1=ALU.mult),
                         r=["cb", "cvec"], w=["cb"])
                    P.op("dve", lambda e, kc=kc: e.tensor_tensor(out=sc[:, kc * 512:(kc + 1) * 512], in0=ps[2][:, :], in1=cb, op=ALU.add),
                         r=["ps2", "cb"], w=["sc"])
            P.op("dve", lambda e, nkc=nkc: e.tensor_reduce(out=sm[:, 3:4], in_=sm[:, 32:32 + nkc], axis=AX.X, op=ALU.max), r=["mx"], w=["sm"])
            P.op("dve", lambda e: e.tensor_scalar(out=sm[:, 3:4], in0=sm[:, 3:4], scalar1=1.001, scalar2=1e-6, op0=ALU.mult, op1=ALU.add),
                 r=["sm"], w=["sm"])
            P.op("dve", lambda e: e.tensor_scalar(out=hw[:, 0:NIT + 2], in0=cvec[:, 32:32 + NIT + 2], scalar1=sm[:, 3:4], scalar2=None, op0=ALU.mult),
                 r=["sm", "cvec"], w=["hw"])
            P.op("dve", lambda e: e.memset(sm[:, 0:1], 0.0), r=["sm"], w=["sm"])
            for it in range(NIT):
                P.op("dve", lambda e, Ls=Ls: e.tensor_scalar(out=NM[:, 0:Ls], in0=sc[:, 0:Ls], scalar1=sm[:, 0:1], scalar2=None,
                                                             op0=ALU.is_ge, op1=ALU.add, accum_out=sm[:, 1:2]),
                     r=["sc", "sm"], w=["NM", "sm"])
                P.op("dve", lambda e, it=it: e.tensor_scalar(out=sm[:, 2:3], in0=sm[:, 1:2], scalar1=256.0, scalar2=hw[:, it:it + 1],
                                                             op0=ALU.is_ge, op1=ALU.mult), r=["sm", "hw"], w=["sm"])
                P.op("dve", lambda e, it=it: e.tensor_scalar(out=sm[:, 0:1], in0=sm[:, 2:3], scalar1=sm[:, 0:1], scalar2=hw[:, it + 1:it + 2],
                                                             op0=ALU.add, op1=ALU.subtract), r=["sm", "hw"], w=["sm"])
            P.op("dve", lambda e: e.tensor_tensor(out=sm[:, 4:5], in0=sm[:, 0:1], in1=hw[:, NIT:NIT + 1], op=ALU.subtract), r=["sm", "hw"], w=["sm"])
            P.op("dve", lambda e, Ls=Ls: e.tensor_scalar(out=NM[:, 0:Ls], in0=sc[:, 0:Ls], scalar1=sm[:, 4:5], scalar2=NEG,
                                                         op0=ALU.is_lt, op1=ALU.mult), r=["sc", "sm"], w=["NM"])
            for b in (5, 6, 7):
                P.op("dve", lambda e, b=b: e.memset(ps[b][:, :], 0.0), w=[f"ps{b}"])
            nkb = 4 * nkc
            for kb in range(nkb):
                ksl, krk = kb // 4, kb % 4
                for n2 in range(2):
                    b = 3 + n2
                    pp = n2
                    for nl in range(2):
                        n = 2 * n2 + nl
                        P.op("pe", lambda e, b=b, nl=nl, n=n, ksl=ksl, krk=krk, qs=qs: e.matmul(
                            ps[b][:, nl * 256:(nl + 1) * 256], lhsT=KT[:, n, ksl, krk, :], rhs=self.qT[:, 2 * n:2 * n + 2, qs],
                            start=True, stop=False), r=ktok + kitok + vtok + ["qT"], w=[f"ps{b}"])
                        P.op("pe", lambda e, b=b, nl=nl, kb=kb: e.matmul(
                            ps[b][:, nl * 256:(nl + 1) * 256], lhsT=NM[:, kb * 128:(kb + 1) * 128], rhs=self.identb[:, :],
                            start=False, stop=True), r=["NM", "identb"], w=[f"ps{b}"])
                    P.op("act", lambda e, b=b, pp=pp: e.activation(out=PTb[pp], in_=ps[b][:, :], func=AF.Exp, scale=ATT),
                         r=[f"ps{b}"], w=[f"PT{pp}"])
                for n2 in range(2):
                    pp = n2
                    for nl in range(2):
                        n = 2 * n2 + nl
                        for g in range(2):
                            h = 2 * n + g
                            ob, oc = 5 + h // 3, (h % 3) * 129
                            P.op("pe", lambda e, ob=ob, oc=oc, pp=pp, nl=nl, g=g, n=n, ksl=ksl, krk=krk: e.matmul(
                                ps[ob][:, oc:oc + 129], lhsT=PTb[pp][:, nl * 256 + g * 128:nl * 256 + g * 128 + 128],
                                rhs=VA[:, ksl, krk, n * 129:(n + 1) * 129], start=False, stop=False, skip_group_check=True),
                                r=[f"PT{pp}"] + ktok + kitok + vtok, w=[f"ps{ob}"])
            for h in range(8):
                ob, oc = 5 + h // 3, (h % 3) * 129
                P.op("dve", lambda e, h=h, ob=ob, oc=oc: e.reciprocal(out=sm[:, 8 + h:9 + h], in_=ps[ob][:, oc + 128:oc + 129]),
                     r=[f"ps{ob}"], w=[f"rc{h}"])
                P.op("dve", lambda e, h=h, ob=ob, oc=oc: e.tensor_scalar(out=ao[:, h, :], in0=ps[ob][:, oc:oc + 128], scalar1=sm[:, 8 + h:9 + h],
                                                                         scalar2=None, op0=ALU.mult), r=[f"ps{ob}", f"rc{h}"], w=[f"ao{h}"])
            for h in range(8):
                b = 3 + (h // 4)
                P.op("pe", lambda e, b=b, h=h: e.transpose(ps[b][:, :].bitcast(BF16)[:, (h % 4) * 128:(h % 4 + 1) * 128], ao[:, h, :], self.identb[:, 0:128]),
                     r=[f"ao{h}", "identb"], w=[f"ps{b}"])
                if h % 4 == 3:
                    P.op("act", lambda e, b=b, h=h, qs=qs: e.activation(
                        out=self.mixT[:, 8 + h - 3:8 + h + 1, qs], in_=ps[b][:, :].bitcast(BF16)[:, 0:512].rearrange("p (a b) -> p a b", a=4),
                        func=AF.Identity), r=[f"ps{b}"], w=[f"mixa{s}_{h // 4}"])

    def sample_attention(self):
        P = self.P
        ps = self.ps
        nc = self.nc
        P.barrier()
        NB = 65
        CR = getattr(self, "cache_rows", 327680)
        ck = self.din("cache_k", [CR, 512])
        cv_ = self.din("cache_v", [CR, 512])
        cki = self.din("cache_k_idx", [CR, 64])
        pt_d = self.din("pt", [4, 64], I32)
        KTs = self.view(17024, [4, 32, 128], BF16)
        VAs = self.view(25216, [32, 4, 129], BF16)
        kiTs = self.view(46688, [64, 128], BF16, parts=64)
        scT = self.view(37888, [4, NB, 4])
        selT = self.view(38928, [4, NB, 4], BF16)
        tmpc = self.view(39448, [4, NB, 4])
        rl = self.view(40488, [16, 32])
        PTs = [self.view(41000 + 256 * j, [16, 32], BF16) for j in range(2)]
        kst = [self.view(41512 + 256 * j, [512], BF16) for j in range(4)]
        kist = [self.view(42536 + 32 * j, [64], BF16) for j in range(4)]
        idx = self.view(42664, [256], I32)
        idxf = self.view(42920, [256])
        vst = [self.view(45812 + 256 * j, [512], BF16) for j in range(3)]
        self.vq = 0
        Wb = self.view(44788, [16, 8])
        wd_ = self.view(44916, [16, 8], parts=16)
        aos = self.view(45300, [4, 128], BF16, parts=8)
        st = self.small("sst", [128, 64])
        cvec = self.cvec
        C_IDX = (64.0 ** -0.5) * (8.0 ** -0.5)
        ATT = 128.0 ** -0.5
        NIT = 30
        M0 = 1024.0

        P.op("sp", lambda e: e.dma_start(out=idx, in_=pt_d.rearrange("b g -> (b g)").partition_broadcast(128)), w=["idx"], dma="ldpt")
        P.op("dve", lambda e: e.tensor_copy(out=idxf, in_=idx), r=["idx"], w=["idxf"])
        P.op("dve", lambda e: e.tensor_scalar(out=idxf, in0=idxf, scalar1=128.0, scalar2=cvec[:, 56:57], op0=ALU.mult, op1=ALU.add),
             r=["idxf", "cvec"], w=["idxf"])
        P.op("dve", lambda e: e.tensor_copy(out=idx, in_=idxf), r=["idxf"], w=["idx"])
        P.op("dve", lambda e: e.tensor_tensor(out=wd_, in0=self.wtok[0:16, 8:9, :].broadcast_to([16, 16, 8]),
                                              in1=self.ident[0:16, 0:16].unsqueeze(2).broadcast_to([16, 16, 8]), op=ALU.mult),
             r=["wtok", "ident"], w=["wd_"])
        P.op("pe", lambda e: e.matmul(ps[0][:, 0:128], lhsT=self.ones[0:16, :], rhs=wd_.rearrange("p a b -> p (a b)"), start=True, stop=True),
             r=["wd_", "ones"], w=["ps0"])
        P.op("act", lambda e: e.activation(out=Wb.rearrange("p a b -> p (a b)"), in_=ps[0][:, 0:128], func=AF.Identity, scale=C_IDX),
             r=["ps0"], w=["Wb"])
        self.gq = 0

        def gather(dst, src, col, tok, key):
            P.op("pool", lambda e: e.indirect_dma_start(out=dst, out_offset=None, in_=src,
                                                        in_offset=bass.IndirectOffsetOnAxis(ap=idx[:, col:col + 1], axis=0)),
                 r=["idx"], w=[tok], dma=key)

        for b in range(4):
            qi_b = self.qiT[:, :, NPB + 4 * b:NPB + 4 * b + 4]
            for pg0 in range(0, 64, 8):
                bk = 1 + (pg0 // 8) % 2
                for j in range(8):
                    pg = pg0 + j
                    sl_ = self.gq % 4
                    self.gq += 1
                    gather(kist[sl_], cki, b * 64 + pg, f"kist{sl_}", f"kist{sl_}")
                    P.op("pe", lambda e, bk=bk, j=j, sl_=sl_: e.transpose(ps[bk][0:64, :].bitcast(BF16)[:, j * 128:(j + 1) * 128],
                                                                          kist[sl_], self.identb[:, 0:128]),
                         r=[f"kist{sl_}", "identb"], w=[f"ps{bk}"])
                P.op("act", lambda e, bk=bk, pg0=pg0: e.activation(out=kiTs[:, pg0:pg0 + 8, :],
                                                                   in_=ps[bk][0:64, :].bitcast(BF16).rearrange("p (a b) -> p a b", a=8),
                                                                   func=AF.Identity), r=[f"ps{bk}"], w=["kiTs"])
            for g0 in range(0, 64, 16):
                bk = 3 + (g0 // 16) % 2
                for j in range(16):
                    P.op("pe", lambda e, bk=bk, j=j, g0=g0, qi_b=qi_b: e.matmul(ps[bk][:, j * 32:(j + 1) * 32], lhsT=kiTs[:, g0 + j, :], rhs=qi_b,
                                                                               start=True, stop=True), r=["kiTs", "qiT"], w=[f"ps{bk}"])
                P.op("act", lambda e, bk=bk: e.activation(out=rl.rearrange("p a b -> p (a b)"), in_=ps[bk][:, :], func=AF.Relu),
                     r=[f"ps{bk}"], w=["rl"])
                wq = Wb[:, 4 * b:4 * b + 4, :].rearrange("p q h -> p h q").unsqueeze(1).broadcast_to([128, 16, 8, 4])
                P.op("dve", lambda e, wq=wq: e.tensor_tensor(out=rl.rearrange("p a (h q) -> p a h q", h=8), in0=rl.rearrange("p a (h q) -> p a h q", h=8),
                                                             in1=wq, op=ALU.mult), r=["rl", "Wb"], w=["rl"])
                P.op("dve", lambda e, b=b, g0=g0: e.tensor_reduce(out=scT[:, b, g0:g0 + 16, :], in_=rl.rearrange("p a (h q) -> p a q h", h=8),
                                                                  axis=AX.X, op=ALU.add), r=["rl"], w=["scT"])
            P.op("pe", lambda e, qi_b=qi_b: e.matmul(ps[5][0:16, 0:32], lhsT=self.kis[:, 0:16], rhs=qi_b, start=True, stop=True),
                 r=["kis", "qiT"], w=["ps5"])
            P.op("act", lambda e: e.activation(out=rl[0:16, 0, :], in_=ps[5][0:16, 0:32], func=AF.Relu), r=["ps5"], w=["rl"])
            wq1 = Wb[0:16, 4 * b:4 * b + 4, :].rearrange("p q h -> p h q")
            P.op("dve", lambda e, wq1=wq1: e.tensor_tensor(out=rl[0:16, 0, :].rearrange("p (h q) -> p h q", h=8),
                                                           in0=rl[0:16, 0, :].rearrange("p (h q) -> p h q", h=8), in1=wq1, op=ALU.mult),
                 r=["rl", "Wb"], w=["rl"])
            P.op("dve", lambda e, b=b: e.tensor_copy(out=scT[:, b, 64, :], in_=cvec[:, 64 + 4 * b:68 + 4 * b]), r=["cvec"], w=["scT"])
            P.op("dve", lambda e, b=b: e.tensor_reduce(out=tmpc[0:16, b, 64, :], in_=rl[0:16, 0, :].rearrange("p (h q) -> p q h", h=8),
                                                       axis=AX.X, op=ALU.add), r=["rl"], w=["tmpc"])
            P.op("dve", lambda e, b=b: e.tensor_tensor(out=scT[0:16, b, 64, :], in0=scT[0:16, b, 64, :], in1=tmpc[0:16, b, 64, :], op=ALU.add),
                 r=["tmpc", "scT"], w=["scT"])
        mid = st[:, 0:16]
        P.op("dve", lambda e: e.memset(mid, 0.0), w=["st"])
        for it in range(NIT):
            hw0, hw1 = M0 * 0.5 ** it, M0 * 0.5 ** (it + 1)
            P.op("dve", lambda e: e.tensor_tensor(out=tmpc, in0=scT, in1=mid.rearrange("p (b q) -> p b q", b=4).unsqueeze(2).broadcast_to([128, 4, NB, 4]),
                                                  op=ALU.is_ge), r=["scT", "st"], w=["tmpc"])
            P.op("dve", lambda e: e.tensor_reduce(out=st[:, 16:32].rearrange("p (b q) -> p b q", b=4), in_=tmpc.rearrange("p b k q -> p b q k"),
                                                  axis=AX.X, op=ALU.add), r=["tmpc"], w=["cntp"])
            P.op("pe", lambda e: e.matmul(ps[6][:, 0:16], lhsT=self.ones[:], rhs=st[:, 16:32], start=True, stop=True), r=["cntp", "ones"], w=["ps6"])
            P.op("dve", lambda e, hw0=hw0: e.tensor_scalar(out=st[:, 32:48], in0=ps[6][:, 0:16], scalar1=256.0, scalar2=hw0, op0=ALU.is_ge, op1=ALU.mult),
                 r=["ps6"], w=["u"])
            P.op("dve", lambda e, hw1=hw1: e.scalar_tensor_tensor(out=mid, in0=st[:, 32:48], scalar=-hw1, in1=mid, op0=ALU.add, op1=ALU.add),
                 r=["u", "st"], w=["st"])
        P.op("dve", lambda e: e.tensor_scalar(out=mid, in0=mid, scalar1=-(M0 * 0.5 ** NIT), scalar2=None, op0=ALU.add), r=["st"], w=["st"])
        P.op("dve", lambda e: e.tensor_tensor(out=selT, in0=scT, in1=mid.rearrange("p (b q) -> p b q", b=4).unsqueeze(2).broadcast_to([128, 4, NB, 4]),
                                              op=ALU.is_ge), r=["scT", "st"], w=["selT"])
        for b in range(4):
            P.op("dve", lambda e: e.memset(ps[7][:, :], 0.0), w=["ps7"])
            P.op("dve", lambda e: e.memset(ps[0][:, :], 0.0), w=["ps0"])
            OB = lambda n: (7 if n < 2 else 0)
            OC = lambda n: (n % 2) * 129
            q_b = [self.qT[:, 2 * n:2 * n + 2, NPB + 4 * b:NPB + 4 * b + 4] for n in range(4)]
            for half in range(2):
                for pl in range(32):
                    pg = 32 * half + pl
                    sl_ = self.gq % 4
                    self.gq += 1
                    gather(kst[sl_], ck, b * 64 + pg, f"kst{sl_}", f"kst{sl_}")
                    vsl = self.vq % 3
                    self.vq += 1
                    gather(vst[vsl], cv_, b * 64 + pg, f"vst{vsl}", f"vst{vsl}")
                    P.op("dve", lambda e, pl=pl, vsl=vsl: e.tensor_copy(out=VAs[:, pl, :, 0:128], in_=vst[vsl].rearrange("p (n d) -> p n d", n=4)),
                         r=[f"vst{vsl}"], w=[f"VAs{pl}"])
                    bk = 1 + pl % 2
                    for n in range(4):
                        P.op("pe", lambda e, bk=bk, n=n, sl_=sl_: e.transpose(ps[bk][:, :].bitcast(BF16)[:, n * 128:(n + 1) * 128],
                                                                             kst[sl_][:, n * 128:(n + 1) * 128], self.identb[:, 0:128]),
                             r=[f"kst{sl_}", "identb"], w=[f"ps{bk}"])
                    P.op("act" if pl % 2 == 0 else "dve", (lambda e, bk=bk, pl=pl: e.activation(
                        out=KTs[:, :, pl, :], in_=ps[bk][:, :].bitcast(BF16)[:, 0:512].rearrange("p (a b) -> p a b", a=4), func=AF.Identity))
                        if pl % 2 == 0 else (lambda e, bk=bk, pl=pl: e.tensor_copy(
                            out=KTs[:, :, pl, :], in_=ps[bk][:, :].bitcast(BF16)[:, 0:512].rearrange("p (a b) -> p a b", a=4))),
                        r=[f"ps{bk}"], w=[f"KTs{pl}"])
                for g0 in range(0, 32, 16):
                    bk = 3 + (g0 // 16) % 2
                    pp = (g0 // 16) % 2
                    for j in range(16):
                        for n in range(4):
                            P.op("pe", lambda e, bk=bk, j=j, n=n, g0=g0, qn=q_b[n]: e.matmul(ps[bk][:, j * 32 + n * 8:j * 32 + n * 8 + 8], lhsT=KTs[:, n, g0 + j, :],
                                                                                 rhs=qn, start=True, stop=True),
                                 r=[f"KTs{g0 + j}", "qT"], w=[f"ps{bk}"])
                    P.op("act", lambda e, bk=bk, pp=pp: e.activation(out=PTs[pp].rearrange("p a b -> p (a b)"), in_=ps[bk][:, :], func=AF.Exp, scale=ATT),
                         r=[f"ps{bk}"], w=[f"PTs{pp}"])
                    sv = selT[:, b, 32 * half + g0:32 * half + g0 + 16, :].unsqueeze(2).broadcast_to([128, 16, 8, 4])
                    P.op("dve", lambda e, pp=pp, sv=sv: e.tensor_tensor(out=PTs[pp].rearrange("p a (c q) -> p a c q", c=8),
                                                                        in0=PTs[pp].rearrange("p a (c q) -> p a c q", c=8), in1=sv, op=ALU.mult),
                         r=[f"PTs{pp}", "selT"], w=[f"PTs{pp}"])
                    for j in range(16):
                        for n in range(4):
                            P.op("pe", lambda e, pp=pp, j=j, n=n, g0=g0: e.matmul(ps[OB(n)][0:8, OC(n):OC(n) + 129], lhsT=PTs[pp][:, j, n * 8:(n + 1) * 8],
                                                                                 rhs=VAs[:, g0 + j, n, :],
                                                                                 start=False, stop=False, skip_group_check=True),
                                 r=[f"PTs{pp}"] + [f"VAs{p_}" for p_ in range(32)], w=[f"ps{OB(n)}"])
            for n in range(4):
                P.op("pe", lambda e, n=n, qn=q_b[n]: e.matmul(ps[5][0:16, n * 8:(n + 1) * 8], lhsT=self.kts[:, n, :], rhs=qn, start=True, stop=True),
                     r=["kts", "qT"], w=["ps5"])
            P.op("act", lambda e: e.activation(out=PTs[0][0:16, 0, :], in_=ps[5][0:16, 0:32], func=AF.Exp, scale=ATT), r=["ps5"], w=["PTs0"])
            P.op("dve", lambda e, b=b: e.tensor_tensor(out=PTs[0][0:16, 0, :].rearrange("p (c q) -> p c q", c=8),
                                                       in0=PTs[0][0:16, 0, :].rearrange("p (c q) -> p c q", c=8),
                                                       in1=selT[0:16, b, 64:65, :].broadcast_to([16, 8, 4]), op=ALU.mult),
                 r=["PTs0", "selT"], w=["PTs0"])
            for n in range(4):
                P.op("pe", lambda e, n=n: e.matmul(ps[OB(n)][0:8, OC(n):OC(n) + 129], lhsT=PTs[0][0:16, 0, n * 8:(n + 1) * 8],
                                                   rhs=self.vs[:, n, :], start=False, stop=False, skip_group_check=True),
                     r=["PTs0", "vs"], w=[f"ps{OB(n)}"])
            for n in range(4):
                P.op("dve", lambda e, n=n: e.reciprocal(out=st[0:8, 48 + n:49 + n], in_=ps[OB(n)][0:8, OC(n) + 128:OC(n) + 129]), r=[f"ps{OB(n)}"], w=["rec"])
                P.op("dve", lambda e, n=n: e.tensor_scalar(out=aos[:, n, :], in0=ps[OB(n)][0:8, OC(n):OC(n) + 128], scalar1=st[0:8, 48 + n:49 + n],
                                                           scalar2=None, op0=ALU.mult), r=[f"ps{OB(n)}", "rec"], w=["aos"])
            for n in range(4):
                P.op("pe", lambda e, n=n: e.transpose(ps[6][:, :].bitcast(BF16)[:, n * 8:(n + 1) * 8], aos[:, n, :], self.identb[0:8, 0:8]),
                     r=["aos", "identb"], w=["ps6"])
            P.op("act", lambda e, b=b: e.activation(out=self.mixT[:, 8:16, NPB + 4 * b:NPB + 4 * b + 4],
                                                    in_=ps[6][:, :].bitcast(BF16)[:, 0:32].rearrange("p (n g q) -> p (n g) q", n=4, g=2),
                                                    func=AF.Identity), r=["ps6"], w=[f"mixs{b}"])

    def dump_sample_debug(self):
        P = self.P
        d1 = self.dout("dbg_scT", [128, 1040])
        d2 = self.dout("dbg_st", [128, 64])
        d3 = self.dout("dbg_selT", [128, 1040], BF16)
        d4 = self.dout("dbg_mixs", [128, KC, NS], BF16)
        d5 = self.dout("dbg_Wb", [128, 128])
        d6 = self.dout("dbg_qis", [64, 8, NS], BF16)
        d7 = self.dout("dbg_qs", [128, 8, NS], BF16)
        scT = self.view(37888, [1040]); selT = self.view(38928, [1040], BF16); Wb = self.view(44788, [128])
        P.barrier()
        P.op("sp", lambda e: e.dma_start(out=d1, in_=scT), w=["dd1"], dma="dd1")
        P.op("sp", lambda e: e.dma_start(out=d3, in_=selT), w=["dd3"], dma="dd3")
        P.op("sp", lambda e: e.dma_start(out=d4, in_=self.mixT[:, :, NPB:NPB + NS]), w=["dd4"], dma="dd4")
        P.op("sp", lambda e: e.dma_start(out=d5, in_=Wb), w=["dd5"], dma="dd5")
        P.op("sp", lambda e: e.dma_start(out=d6, in_=self.qiT[:, :, NPB:NPB + NS]), w=["dd6"], dma="dd6")
        P.op("sp", lambda e: e.dma_start(out=d7, in_=self.qT[:, :, NPB:NPB + NS]), w=["dd7"], dma="dd7")
        d8 = self.dout("dbg_KTs", [128, 4, 32, 128], BF16)
        d9 = self.dout("dbg_VAs", [128, 32, 516], BF16)
        d10 = self.dout("dbg_aos", [8, 512], BF16)
        P.op("sp", lambda e: e.dma_start(out=d8, in_=self.view(17024, [4, 32, 128], BF16)), w=["dd8"], dma="dd8")
        P.op("sp", lambda e: e.dma_start(out=d9, in_=self.view(25216, [32, 516], BF16)), w=["dd9"], dma="dd9")
        P.op("sp", lambda e: e.dma_start(out=d10, in_=self.view(45300, [512], BF16, parts=8)), w=["dd10"], dma="dd10")
        self.dbg_toks = ["dd1", "dd3", "dd4", "dd5", "dd6", "dd7", "dd8", "dd9", "dd10"]

    def rnn_fixup(self):
        P = self.P
        cg = self.view(37888, [4, 8, 8, 2])
        seq = self.view(38400, [8, 32, 2])
        hend = self.view(38912, [8, 32])
        hin = self.view(39168, [8, 8])
        tmp = self.view(39232, [8, 32])
        P.op("sp", lambda e: e.dma_start(out=cg.rearrange("p r a b c -> p r (a b c)"), in_=self.c_out.rearrange("(r p) c -> p r c", p=128)),
             r=["c_out"], w=["cg"], dma="ldc")
        for s in range(8):
            k = s // 2
            if s % 2 == 0:
                P.op("dve", lambda e, s=s, k=k: e.tensor_copy(out=seq[:, :, 8 * k:8 * k + 4, :], in_=cg[:, :, :, s, :].rearrange("p r c t -> p c r t")),
                     r=["cg"], w=["seq"])
            else:
                for r in range(4):
                    P.op("dve", lambda e, s=s, k=k, r=r: e.tensor_copy(out=seq[:, :, 8 * k + 7 - r, :], in_=cg[:, r, :, s, :]), r=["cg"], w=["seq"])
        for ct in range(8):
            P.op("dve", lambda e, ct=ct: e.tensor_tensor_scan(out=hend[:, ct, :], data0=seq[:, ct, :, 0], data1=seq[:, ct, :, 1], initial=0.0,
                                                              op0=ALU.mult, op1=ALU.add), r=["seq"], w=["hend"])
        P.op("dve", lambda e: e.tensor_copy(out=self.stt[:, :, 15], in_=hend[:, :, 31]), r=["hend"], w=["stt"])
        P.op("sp", lambda e: e.dma_start(out=self.st_out, in_=self.stt), r=["stt"], w=[self.otok("st")], dma="st_out")
        for s in range(8):
            P.op("dve", lambda e, s=s: e.tensor_tensor(out=tmp, in0=hend, in1=self.sel[:, s:s + 1, :].broadcast_to([128, 8, 32]), op=ALU.mult),
                 r=["hend", "sel"], w=["tmp"])
            P.op("dve", lambda e, s=s: e.tensor_reduce(out=hin[:, :, s], in_=tmp, axis=AX.X, op=ALU.add), r=["tmp"], w=["hin"])
        for ct in range(8):
            for s in range(8):
                sl = slice(s * 128, (s + 1) * 128)
                P.op("dve", lambda e, ct=ct, s=s, sl=sl: e.scalar_tensor_tensor(
                    out=self.mixT[:, ct, sl], in0=self.PG[:, ct, sl], scalar=hin[:, ct, s:s + 1], in1=self.mixT[:, ct, sl],
                    op0=ALU.mult, op1=ALU.add), r=[f"pg{ct}", "hin", f"mix{ct}"], w=[f"mix{ct}"])

    def phase_d(self, w_out):
        P = self.P
        ps = self.ps
        NQ = NPB + NS
        P.barrier()
        self.tiles = [(0, 512), (512, 512), (1024, NS)]
        res2 = self.view(17024, [KC, NQ])
        wo = [self.view(34048 + 4096 * sl_, [KC, 512], BF16) for sl_ in range(2)]
        for ti, (t0, n, s0) in enumerate(((0, 512, 0), (512, 512, 512), (1024, NS, SMP0))):
            P.op("sp", lambda e, t0=t0, n=n, s0=s0: e.dma_start(out=res2[:, :, t0:t0 + n], in_=self.res_spill[:, :, s0:s0 + n]),
                 r=["spill"], w=[f"res{ti}_{dc}" for dc in range(KC)], dma=f"x{ti}")
        wv = w_out.rearrange("(kc p) n -> p kc n", p=128)
        for j in range(4):
            P.op("pool", lambda e, j=j: e.dma_start(out=wo[j % 2][:, :, :], in_=wv[:, :, j * 512:(j + 1) * 512]),
                 w=[f"wo{j % 2}"], dma=f"wo{j % 2}")
            for cc in range(4):
                dc = 4 * j + cc
                for ti, (t0, n) in enumerate(self.tiles):
                    b = (dc * 3 + ti) % 6
                    for kc in range(KC):
                        P.op("pe", lambda e, b=b, n=n, j=j, kc=kc, cc=cc, t0=t0: e.matmul(
                            ps[b][:, 0:n], lhsT=wo[j % 2][:, kc, cc * 128:(cc + 1) * 128], rhs=self.mixT[:, kc, t0:t0 + n],
                            start=(kc == 0), stop=(kc == KC - 1)),
                            r=[f"wo{j % 2}"] + [f"mix{kc}" if kc < 8 else "mixa"], w=[f"ps{b}"])
                    P.op("dve", lambda e, b=b, n=n, dc=dc, t0=t0: e.tensor_tensor(out=res2[:, dc, t0:t0 + n], in0=ps[b][:, 0:n],
                                                                                 in1=res2[:, dc, t0:t0 + n], op=ALU.add),
                         r=[f"ps{b}", f"res{ti}_{dc}"], w=[f"res{ti}_{dc}"])
        self.res = res2
        self.xb = self.view(0, [KC, NQ], BF16)
        self.actb = [self.view(8512 + 2128 * s_, [4, NQ], BF16) for s_ in range(2)]
        self.sg = [self.view(12768 + 256 * s_, [512], BF16) for s_ in range(2)]
        self.s1 = self.view(13280, [512])
        self.mean = self.view(13792, [512])
        self.rstd = self.view(14304, [512])
        self.sqs = self.view(14816, [KC, 128])
        self.wgu = [self.view(34048 + 4096 * s_, [2, KC, 256], BF16) for s_ in range(2)]
        self.wd = [self.view(42240 + 4096 * s_, [4, D], BF16) for s_ in range(2)]
        self.weights_nobarrier = False
        P.barrier()

    def finish(self, out_tokens):
        P = self.P
        P.op("sp", lambda e: None, r=out_tokens)
        P.emit(self.es)
        self.es.close()
        return self.nc


def build(stage=99, upto=None, skip_ffn=False, cache_rows=None):
    B = Builder(stage, upto, skip_ffn)
    P = B.P
    B.setup_common()
    B.load_x()
    if not skip_ffn:
        w_gu1 = B.din("ffn1_w_gu", [D, 2 * DFF])
        w_d1 = B.din("ffn1_w_down", [DFF, D])
        B.ffn(w_gu1, w_d1)
    B.layer_norm(0, 16)
    if stage == 1:
        dbg = B.dout("dbg", [128, KC, NT])
        toks = []
        for ti, (t0, n) in enumerate(TILES):
            P.op("sp", lambda e, t0=t0, n=n: e.dma_start(out=dbg[:, :, t0:t0 + n], in_=B.res[:, :, t0:t0 + n]),
                 r=[f"res{ti}_{dc}" for dc in range(KC)], w=[f"dbg{ti}"], dma=f"dbg{ti}")
            toks.append(f"dbg{ti}")
        return B.finish(toks)
    w_in = B.din("w_in", [D, DIN])
    B.phase_b(w_in)
    if stage == 2:
        toks = list(B.otoks)
        dbg_m = B.dout("dbg_mix", [128, KC, NPB + NS], BF16)
        dbg_pg = B.dout("dbg_pg", [128, 8, NPB + NS], BF16)
        dbg_q = B.dout("dbg_q", [128, 8, NPB + NS], BF16)
        dbg_qi = B.dout("dbg_qi", [64, 8, NPB + NS], BF16)
        dbg_w = B.dout("dbg_w", [128, 72])
        dbg_g = B.dout("dbg_g", [1280, 1024], BF16)
        dbg_c = B.dout("dbg_c", [512, 128])
        if upto is None or upto.startswith("cc"):
            P.op("sp", lambda e: e.dma_start(out=dbg_m[:, 0:8, :], in_=B.mixT[:, 0:8, :]), r=[f"mix{ct}" for ct in range(8)], w=["d1"], dma="d1")
            P.op("sp", lambda e: e.dma_start(out=dbg_pg[:, :, 0:NPB], in_=B.PG[:, :, 0:NPB]), r=[f"pg{ct}" for ct in range(8)], w=["d2"], dma="d2")
            P.op("sp", lambda e: e.dma_start(out=dbg_q, in_=B.qT), r=["qT"], w=["d3"], dma="d3")
            P.op("sp", lambda e: e.dma_start(out=dbg_qi, in_=B.qiT), r=["qiT"], w=["d4"], dma="d4")
        else:
            for i, dd in enumerate((dbg_m, dbg_pg, dbg_q, dbg_qi)):
                P.op("sp", lambda e, dd=dd: e.dma_start(out=dd[0:64, 0, 0:256], in_=B.identb[0:64, :]), r=["identb"], w=[f"d{i+1}"], dma=f"d{i+1}")
        if upto is None or upto.startswith("cc"):
            P.op("sp", lambda e: e.dma_start(out=dbg_w, in_=B.wtok[:].rearrange("p a b -> p (a b)")), r=["wtok"], w=["d5"], dma="d5")
        else:
            P.op("sp", lambda e: e.dma_start(out=dbg_w[0:64, 0:64], in_=B.ident[0:64, 0:64]), r=["ident"], w=["d5"], dma="d5")
        if upto is None or upto.startswith("cc"):
            gsb = B.view(17024, [16, 1024], BF16)
            for i in range(10):
                r0 = 128 * i
                P.op("sp", lambda e, r0=r0: e.dma_start(out=gsb[:, 0, :], in_=B.g_out[0][r0:r0 + 128, :]), r=["g_out0"], w=["gsb"], dma="d6a")
                P.op("sp", lambda e, r0=r0: e.dma_start(out=dbg_g[r0:r0 + 128, :], in_=gsb[:, 0, :]), r=["gsb"], w=["d6"], dma="d6")
            csb = B.view(17024 + 8192, [4, 128])
            P.op("sp", lambda e: e.dma_start(out=csb, in_=B.c_out.rearrange("(r p) c -> p r c", p=128)), r=["c_out"], w=["csb"], dma="d7a")
            P.op("sp", lambda e: e.dma_start(out=dbg_c.rearrange("(r p) c -> p r c", p=128), in_=csb), r=["csb"], w=["d7"], dma="d7")
        else:
            P.op("sp", lambda e: e.dma_start(out=dbg_c[0:128, :], in_=B.ones[:]), r=["ones"], w=["d7"], dma="d7")
            P.op("sp", lambda e: e.dma_start(out=dbg_g[0:128, 0:256], in_=B.identb[:]), r=["identb"], w=["d6"], dma="d6")
        P.op("sp", lambda e: e.dma_start(out=B.st_out, in_=B.stt), r=["stt"], w=["d8"], dma="d8")
        return B.finish(toks + ["d1", "d2", "d3", "d4", "d5", "d6", "d7", "d8"])
    B.phase_c()
    if cache_rows is not None:
        B.cache_rows = cache_rows
    B.sample_attention()
    if cache_rows is not None:
        B.dump_sample_debug()
    w_out = B.din("w_out", [D, D])
    B.phase_d(w_out)
    B.layer_norm(32, 48)
    if not skip_ffn:
        w_gu2 = B.din("ffn2_w_gu", [D, 2 * DFF])
        w_d2 = B.din("ffn2_w_down", [DFF, D])
        B.ffn(w_gu2, w_d2)
    B.layer_norm(64, 80, final=True)
    y_out = B.dout("y_out", [128, KC, NPB + NS])
    toks = list(B.otoks) + list(getattr(B, "dbg_toks", []))
    for ti, (t0, n) in enumerate(B.tiles):
        P.op("sp", lambda e, t0=t0, n=n: e.dma_start(out=y_out[:, :, t0:t0 + n], in_=B.res[:, :, t0:t0 + n]),
             r=[f"res{ti}_{dc}" for dc in range(KC)], w=[f"y{ti}"], dma=f"y{ti}")
        toks.append(f"y{ti}")
    return B.finish(toks)


_NC_CACHE = {}


def kernel(**inp):
    inp = {k: np.asarray(v) for k, v in inp.items()}
    if "nc" not in _NC_CACHE:
        _NC_CACHE["nc"] = build()
    nc = _NC_CACHE["nc"]
    sh = prep_shared_inputs(inp)
    maps = []
    for c in range(8):
        m = dict(prep_core_inputs(c, inp))
        m.update(sh)
        maps.append(m)
    res = run_bass_kernel_spmd(nc, maps, core_ids=list(range(8)))
    return assemble(res.results)


def assemble(R):
    NQ = NPB + NS
    y_p = np.zeros((2, 4096, D), np.float32)
    y_s = np.zeros((32, 4, D), np.float32)
    k_p = np.zeros((1, 2, 4096, 4, 128), np.float32)
    v_p = np.zeros((1, 2, 4096, 4, 128), np.float32)
    ki_p = np.zeros((1, 2, 4096, 64), np.float32)
    cv_p = np.zeros((1, 2, 3, DRNN), np.float32)
    h_p = np.zeros((1, 2, DRNN), np.float32)
    k_s = np.zeros((1, 32, 4, 4, 128), np.float32)
    v_s = np.zeros((1, 32, 4, 4, 128), np.float32)
    ki_s = np.zeros((1, 32, 4, 64), np.float32)
    cv_s = np.zeros((1, 32, 3, DRNN), np.float32)
    h_s = np.zeros((1, 32, DRNN), np.float32)
    for c in range(8):
        b, cp = c // 4, c % 4
        r = R[c]
        y = np.asarray(r["y_out"], np.float32).transpose(2, 1, 0).reshape(NQ, D)
        k = np.asarray(r["k_out"], np.float32).transpose(2, 1, 0)
        v = np.asarray(r["v_out"], np.float32).reshape(NQ, 4, 128)
        ki = np.asarray(r["ki_out"], np.float32).T
        st = np.asarray(r["st_out"], np.float32).transpose(2, 1, 0).reshape(20, DRNN)
        for s in range(8):
            j = block_of(cp, s)
            sl = slice(128 * j, 128 * (j + 1))
            ss = slice(128 * s, 128 * (s + 1))
            y_p[b, sl] = y[ss]
            k_p[0, b, sl] = k[ss]
            v_p[0, b, sl] = v[ss]
            ki_p[0, b, sl] = ki[ss]
        if cp == 0:
            cv_p[0, b] = st[0:3]
            h_p[0, b] = st[15]
        q = slice(4 * c, 4 * c + 4)
        y_s[q] = y[NPB:].reshape(4, 4, D)
        k_s[0, q] = k[NPB:].reshape(4, 4, 4, 128)
        v_s[0, q] = v[NPB:].reshape(4, 4, 4, 128)
        ki_s[0, q] = ki[NPB:].reshape(4, 4, 64)
        cv_s[0, q] = st[3:15].reshape(4, 3, DRNN)
        h_s[0, q] = st[16:20]
    return (y_p, y_s, k_p, v_p, ki_p, cv_p, h_p, k_s, v_s, ki_s, cv_s, h_s)


def block_of(cp, s):
    return 8 * (s // 2) + (cp if s % 2 == 0 else 7 - cp)


def fm(v, nchunk):
    return np.ascontiguousarray(np.asarray(v, np.float32).reshape(nchunk, 128).T)


def prep_core_inputs(c, inp):
    b, cp = c // 4, c % 4
    xp = inp["x_prompt"]
    rows = []
    halo = []
    valid = np.ones((128, NH), np.float32)
    qpos = np.zeros((128, 8), np.float32)
    for s in range(8):
        j = block_of(cp, s)
        rows.append(xp[b, 128 * j:128 * (j + 1)])
        if j == 0:
            halo.append(np.zeros((3, D), np.float32))
            valid[:, 3 * s:3 * s + 3] = 0.0
        else:
            halo.append(xp[b, 128 * j - 3:128 * j])
        qpos[:, s] = 128 * j + np.arange(128)
    xs = inp["x_sample"][4 * c:4 * c + 4].reshape(NS, D)
    xall = np.concatenate(rows + halo + [xs], 0)
    xT = np.ascontiguousarray(xall.reshape(NT, KC, 128).transpose(2, 1, 0))
    cvec = np.zeros((128, 96), np.float32)
    cvec[:, 0:8] = qpos
    cvec[:, 8:8 + NH] = valid
    cvec[:, 32:56] = (0.5 ** np.arange(24, dtype=np.float64)).astype(np.float32)[None, :]
    cvec[:, 56] = np.arange(128)
    cm = np.full((128, 4, 4), NEG, np.float32)
    for bb in range(4):
        for q in range(4):
            cm[4 * bb:4 * bb + q + 1, bb, q] = 0.0
    cvec[:, 64:80] = cm.reshape(128, 16)
    sc = np.asarray(inp["state_conv"][0, 4 * c:4 * c + 4], np.float32)
    sconvT = np.ascontiguousarray(sc.reshape(4, 3, 8, 128).transpose(3, 2, 0, 1))
    sr = np.asarray(inp["state_rnn"][0, 4 * c:4 * c + 4], np.float32)
    srnnT = np.ascontiguousarray(sr.reshape(4, 8, 128).transpose(2, 1, 0))
    sel = np.zeros((128, 8, 32), np.float32)
    for s in range(8):
        j = block_of(cp, s)
        if j > 0:
            sel[:, s, j - 1] = 1.0
    pt = np.ascontiguousarray(np.asarray(inp["page_table"][4 * c:4 * c + 4], np.int32))
    return {"xT": xT, "cvec": cvec, "sconvT": sconvT, "srnnT": srnnT, "selT": sel, "pt": pt}


def prep_shared_inputs(inp):
    pvec = np.zeros((128, NPV), np.float32)
    for i, k in enumerate(("ln1_g", "ln1_b", "ln2_g", "ln2_b", "ln3_g", "ln3_b")):
        pvec[:, 16 * i:16 * (i + 1)] = fm(inp[k][0], 16)
    pvec2 = np.zeros((128, NPV2), np.float32)
    cw = np.asarray(inp["conv_w"][0], np.float32)
    pvec2[:, PV2_CONVW:PV2_CONVW + 32] = cw.reshape(4, 8, 128).transpose(2, 1, 0).reshape(128, 32)
    pvec2[:, PV2_CONVB:PV2_CONVB + 8] = fm(inp["conv_b"][0], 8)
    pvec2[:, PV2_BA:PV2_BA + 8] = fm(inp["lru_b_a"][0], 8)
    pvec2[:, PV2_BI:PV2_BI + 8] = fm(inp["lru_b_i"][0], 8)
    pvec2[:, PV2_LAM:PV2_LAM + 8] = fm(inp["lru_lambda"][0], 8)
    eye = np.eye(128, dtype=np.float32)
    sh = {
        "pvec": pvec, "pvec2": pvec2, "ident": eye,
        "identb": np.concatenate([eye, eye], 1).astype(ml_dtypes.bfloat16),
    }
    for k in ("ffn1_w_gu", "ffn1_w_down", "w_in", "w_out", "ffn2_w_gu", "ffn2_w_down", "lru_w_a", "lru_w_i"):
        sh[k] = np.asarray(inp[k][0], np.float32)
    sh["cache_k"] = np.asarray(inp["cache_k"][0], np.float32).reshape(327680, 512)
    sh["cache_v"] = np.asarray(inp["cache_v"][0], np.float32).reshape(327680, 512)
    sh["cache_k_idx"] = np.asarray(inp["cache_k_idx"][0], np.float32).reshape(327680, 64)
    return sh
```
